# Optimizing a Trainium2 kernel written in Bass

```python
import math
import jax, jax.numpy as jnp
from jax import lax
import numpy as np

D_MODEL = 1024
BATCH = 8
SEQ = 2048
DEPTH = 2
DEC_BATCH = 128
DEC_SEQ = 8
PAST_LEN = 16384
PAGE_SIZE = 128

SSM_GROUP_CH = 16
SSM_WIDTH = D_MODEL // 2
SSM_GROUPS = SSM_WIDTH // SSM_GROUP_CH
SSM_STATE = 64
DT_MIN = 1e-3
DT_MAX = 1e-1
GMLP_HEADS = 4
GMLP_CHUNK = 128
GMLP_WIDTH = D_MODEL // 4
GMLP_HEAD_DIM = GMLP_WIDTH // GMLP_HEADS
POOL_WINDOWS = (2, 4, 8, 16)
POOL_GROUPS = len(POOL_WINDOWS)
POOL_WIDTH = D_MODEL // 2
POOL_GROUP_CH = POOL_WIDTH // POOL_GROUPS
POOL_PAST = max(POOL_WINDOWS) - 1
N_BRANCH = 3
IN_COLS = SSM_WIDTH + 2 * GMLP_WIDTH + POOL_WIDTH + N_BRANCH * D_MODEL
D_FF = ((8 * D_MODEL) // 3 + 127) // 128 * 128
FFN_CONV = 3
LN_EPS = 1e-5
ALPHA = (2 * DEPTH) ** 0.25
BETA = (8 * DEPTH) ** -0.25

kernel_name = 'hybrid_s5_gmlp_pool_convffn_step'


def layer_norm(x, g, b):
    xf = x.astype(jnp.float32)
    mu = jnp.mean(xf, axis=-1, keepdims=True)
    var = jnp.mean(jnp.square(xf - mu), axis=-1, keepdims=True)
    y = (xf - mu) * lax.rsqrt(var + LN_EPS) * g.astype(jnp.float32) + b.astype(jnp.float32)
    return y.astype(x.dtype)


def ssm_discretise(a_re, a_im, log_step, b_re, b_im):
    f32 = jnp.float32
    a_re = a_re.astype(f32)
    a_im = a_im.astype(f32)
    dt = jnp.exp(log_step.astype(f32))[:, None]
    mag = jnp.exp(a_re * dt)
    ab_re = mag * jnp.cos(a_im * dt)
    ab_im = mag * jnp.sin(a_im * dt)
    den = jnp.square(a_re) + jnp.square(a_im)
    nr = ab_re - 1.0
    k_re = (nr * a_re + ab_im * a_im) / den
    k_im = (ab_im * a_re - nr * a_im) / den
    b_re = b_re.astype(f32)
    b_im = b_im.astype(f32)
    bb_re = k_re[..., None] * b_re - k_im[..., None] * b_im
    bb_im = k_re[..., None] * b_im + k_im[..., None] * b_re
    return ab_re, ab_im, bb_re, bb_im


def complex_affine_combine(e1, e2):
    a1r, a1i, b1r, b1i = e1
    a2r, a2i, b2r, b2i = e2
    return (a2r * a1r - a2i * a1i,
            a2r * a1i + a2i * a1r,
            a2r * b1r - a2i * b1i + b2r,
            a2r * b1i + a2i * b1r + b2i)


def ssm_branch(u, h0_re, h0_im, a_re, a_im, log_step, b_re, b_im, c_re, c_im, d_skip, w_glu, b_glu):
    f32 = jnp.float32
    nb, t, _ = u.shape
    uf = u.astype(f32).reshape(nb, t, SSM_GROUPS, SSM_GROUP_CH)
    ab_re, ab_im, bb_re, bb_im = ssm_discretise(a_re, a_im, log_step, b_re, b_im)
    bu_re = jnp.einsum('gnc,btgc->btgn', bb_re, uf)
    bu_im = jnp.einsum('gnc,btgc->btgn', bb_im, uf)
    a_r = jnp.broadcast_to(ab_re, bu_re.shape)
    a_i = jnp.broadcast_to(ab_im, bu_im.shape)
    p_re, p_im, h_re, h_im = lax.associative_scan(
        complex_affine_combine, (a_r, a_i, bu_re, bu_im), axis=1)
    if h0_re is not None:
        s_re = h0_re.astype(f32)[:, None]
        s_im = h0_im.astype(f32)[:, None]
        h_re, h_im = (h_re + p_re * s_re - p_im * s_im,
                      h_im + p_re * s_im + p_im * s_re)
    y = (jnp.einsum('gcn,btgn->btgc', c_re.astype(f32), h_re)
         - jnp.einsum('gcn,btgn->btgc', c_im.astype(f32), h_im)
         + d_skip.astype(f32) * uf).reshape(nb, t, SSM_WIDTH)
    z = jax.nn.gelu(y)
    out = z * jax.nn.sigmoid(z @ w_glu.astype(f32) + b_glu.astype(f32))
    return out.astype(u.dtype), h_re[:, -1], h_im[:, -1]


def gmlp_branch(u, v, ln_g, ln_b, w_s, b_s):
    nb, t, _ = v.shape
    vn = layer_norm(v, ln_g, ln_b)
    cl = min(t, GMLP_CHUNK)
    nc = t // cl
    mask = jnp.tril(jnp.ones((cl, cl), dtype=bool))
    w = jnp.where(mask, w_s[:, :cl, :cl], 0)
    vh = vn.reshape(nb, nc, cl, GMLP_HEADS, GMLP_HEAD_DIM)
    s = (jnp.einsum('hts,bcshd->bcthd', w, vh)
         + jnp.transpose(b_s[:, :cl])[None, None, :, :, None])
    return u * s.reshape(nb, t, GMLP_WIDTH), vn


def pool_branch(xp, past, start_pos, w_pool, scale):
    f32 = jnp.float32
    nb, t, _ = xp.shape
    if past is None:
        past = jnp.zeros((nb, POOL_PAST, POOL_WIDTH), xp.dtype)
    xf = jnp.concatenate([past.astype(f32), xp.astype(f32)], axis=1)
    cs = jnp.concatenate([jnp.zeros((nb, 1, POOL_WIDTH), f32),
                          jnp.cumsum(xf, axis=1)], axis=1)
    pos = start_pos + jnp.arange(t, dtype=jnp.int32)
    means = []
    for g, win in enumerate(POOL_WINDOWS):
        ch = slice(g * POOL_GROUP_CH, (g + 1) * POOL_GROUP_CH)
        lo = POOL_PAST + 1 - win
        wsum = cs[:, POOL_PAST + 1:, ch] - cs[:, lo:lo + t, ch]
        cnt = jnp.minimum(pos + 1, win).astype(f32)[None, :, None]
        means.append(wsum / cnt)
    d = (jnp.concatenate(means, axis=-1) - xf[:, POOL_PAST:]).reshape(
        nb, t, POOL_GROUPS, POOL_GROUP_CH)
    y = jnp.einsum('btgc,gcd->btgd', d, w_pool.astype(f32)).reshape(nb, t, POOL_WIDTH)
    y = y * scale.astype(f32)
    return y.astype(xp.dtype), xf[:, -POOL_PAST:].astype(xp.dtype)


def conv_ffn(x, past, w_up, conv_w, conv_b, w_down):
    nb, t, _ = x.shape
    h = x @ w_up
    if past is None:
        past = jnp.zeros((nb, FFN_CONV - 1, 2 * D_FF), h.dtype)
    hc = jnp.concatenate([past.astype(h.dtype), h], axis=1)
    y = conv_b + hc[:, 0:t] * conv_w[0]
    for k in range(1, FFN_CONV):
        y = y + hc[:, k:k + t] * conv_w[k]
    val, gate = y[..., :D_FF], y[..., D_FF:]
    out = (jax.nn.gelu(gate) * val) @ w_down
    return out, hc[:, -(FFN_CONV - 1):]


def token_mix(x, st_re, st_im, st_pool, start_pos, p):
    nb, t, _ = x.shape
    proj = x @ p['w_in']
    o1 = SSM_WIDTH
    o2 = o1 + GMLP_WIDTH
    o3 = o2 + GMLP_WIDTH
    o4 = o3 + POOL_WIDTH
    y_a, h_re, h_im = ssm_branch(proj[..., :o1], st_re, st_im, p['ssm_a_re'], p['ssm_a_im'],
                                 p['ssm_log_step'], p['ssm_b_re'], p['ssm_b_im'],
                                 p['ssm_c_re'], p['ssm_c_im'], p['ssm_d'],
                                 p['ssm_w_glu'], p['ssm_b_glu'])
    y_b, v_rows = gmlp_branch(jax.nn.gelu(proj[..., o1:o2]), jax.nn.gelu(proj[..., o2:o3]),
                              p['gmlp_ln_g'], p['gmlp_ln_b'], p['gmlp_w_s'], p['gmlp_b_s'])
    y_c, pool_buf = pool_branch(proj[..., o3:o4], st_pool, start_pos, p['pool_w'], p['pool_scale'])
    gates = jax.nn.sigmoid(proj[..., o4:].reshape(nb, t, N_BRANCH, D_MODEL) + p['b_gate'])
    merged = (gates[:, :, 0] * (y_a @ p['w_br_ssm'])
              + gates[:, :, 1] * (y_b @ p['w_br_gmlp'])
              + gates[:, :, 2] * (y_c @ p['w_br_pool']))
    return merged @ p['w_o'], h_re, h_im, v_rows, pool_buf


def trunk_layer(x, st_re, st_im, st_pool, st_conv, start_pos, p):
    mix, h_re, h_im, v_rows, pool_buf = token_mix(x, st_re, st_im, st_pool, start_pos, p)
    x1 = layer_norm(ALPHA * x + mix, p['ln1_g'], p['ln1_b'])
    ff, conv_buf = conv_ffn(x1, st_conv, p['ffn_w_up'], p['ffn_conv_w'], p['ffn_conv_b'], p['ffn_w_down'])
    x2 = layer_norm(ALPHA * x1 + ff, p['ln2_g'], p['ln2_b'])
    return x2, h_re, h_im, v_rows, pool_buf, conv_buf


def setup_inputs(seed: int = 0) -> dict:
    key = jax.random.key(seed)
    ks = iter(jax.random.split(key, 40))

    def nrm(shape, scale=1.0):
        return jax.random.normal(next(ks), shape, jnp.float32) * scale

    L = DEPTH
    G, N, C = SSM_GROUPS, SSM_STATE, SSM_GROUP_CH
    F2 = 2 * D_FF
    return {
        'x_prompt': nrm((BATCH, SEQ, D_MODEL)),
        'x_sample': nrm((DEC_BATCH, DEC_SEQ, D_MODEL)),
        'state_ssm_re': nrm((L, DEC_BATCH, G, N), 0.1),
        'state_ssm_im': nrm((L, DEC_BATCH, G, N), 0.1),
        'state_pool': nrm((L, DEC_BATCH, POOL_PAST, POOL_WIDTH)),
        'state_ffn_conv': nrm((L, DEC_BATCH, FFN_CONV - 1, F2)),
        'w_in': nrm((L, D_MODEL, IN_COLS), D_MODEL ** -0.5),
        'b_gate': nrm((L, N_BRANCH, D_MODEL), 0.01),
        'ssm_a_re': -0.5 + nrm((L, G, N), 0.01),
        'ssm_a_im': math.pi * jnp.arange(N, dtype=jnp.float32) + nrm((L, G, N), 0.01),
        'ssm_log_step': jax.random.uniform(next(ks), (L, G), jnp.float32,
                                           math.log(DT_MIN), math.log(DT_MAX)),
        'ssm_b_re': nrm((L, G, N, C), (2 * C) ** -0.5),
        'ssm_b_im': nrm((L, G, N, C), (2 * C) ** -0.5),
        'ssm_c_re': nrm((L, G, C, N), (2 * N) ** -0.5),
        'ssm_c_im': nrm((L, G, C, N), (2 * N) ** -0.5),
        'ssm_d': nrm((L, G, C)),
        'ssm_w_glu': nrm((L, SSM_WIDTH, SSM_WIDTH), SSM_WIDTH ** -0.5),
        'ssm_b_glu': nrm((L, SSM_WIDTH), 0.01),
        'gmlp_ln_g': 1.0 + nrm((L, GMLP_WIDTH), 0.05),
        'gmlp_ln_b': nrm((L, GMLP_WIDTH), 0.01),
        'gmlp_w_s': nrm((L, GMLP_HEADS, GMLP_CHUNK, GMLP_CHUNK), GMLP_CHUNK ** -0.5),
        'gmlp_b_s': 1.0 + nrm((L, GMLP_HEADS, GMLP_CHUNK), 0.1),
        'pool_w': nrm((L, POOL_GROUPS, POOL_GROUP_CH, POOL_GROUP_CH), POOL_GROUP_CH ** -0.5),
        'pool_scale': 1.0 + nrm((L, POOL_WIDTH), 0.1),
        'w_br_ssm': nrm((L, SSM_WIDTH, D_MODEL), SSM_WIDTH ** -0.5),
        'w_br_gmlp': nrm((L, GMLP_WIDTH, D_MODEL), GMLP_WIDTH ** -0.5),
        'w_br_pool': nrm((L, POOL_WIDTH, D_MODEL), POOL_WIDTH ** -0.5),
        'w_o': nrm((L, D_MODEL, D_MODEL), BETA * D_MODEL ** -0.5),
        'ln1_g': 1.0 + nrm((L, D_MODEL), 0.05),
        'ln1_b': nrm((L, D_MODEL), 0.01),
        'ffn_w_up': nrm((L, D_MODEL, F2), D_MODEL ** -0.5),
        'ffn_conv_w': nrm((L, FFN_CONV, F2), FFN_CONV ** -0.5),
        'ffn_conv_b': nrm((L, F2), 0.01),
        'ffn_w_down': nrm((L, D_FF, D_MODEL), BETA * D_FF ** -0.5),
        'ln2_g': 1.0 + nrm((L, D_MODEL), 0.05),
        'ln2_b': nrm((L, D_MODEL), 0.01),
    }


def reference(x_prompt, x_sample, state_ssm_re, state_ssm_im, state_pool, state_ffn_conv,
              w_in, b_gate, ssm_a_re, ssm_a_im, ssm_log_step, ssm_b_re, ssm_b_im,
              ssm_c_re, ssm_c_im, ssm_d, ssm_w_glu, ssm_b_glu, gmlp_ln_g, gmlp_ln_b,
              gmlp_w_s, gmlp_b_s, pool_w, pool_scale, w_br_ssm, w_br_gmlp, w_br_pool, w_o,
              ln1_g, ln1_b, ffn_w_up, ffn_conv_w, ffn_conv_b, ffn_w_down, ln2_g, ln2_b):
    hp = x_prompt
    hs = x_sample
    p_re, p_im, p_pool, p_conv = [], [], [], []
    s_re, s_im, s_v, s_pool, s_conv = [], [], [], [], []
    for l in range(DEPTH):
        p = dict(w_in=w_in[l], b_gate=b_gate[l], ssm_a_re=ssm_a_re[l], ssm_a_im=ssm_a_im[l],
                 ssm_log_step=ssm_log_step[l], ssm_b_re=ssm_b_re[l], ssm_b_im=ssm_b_im[l],
                 ssm_c_re=ssm_c_re[l], ssm_c_im=ssm_c_im[l], ssm_d=ssm_d[l],
                 ssm_w_glu=ssm_w_glu[l], ssm_b_glu=ssm_b_glu[l],
                 gmlp_ln_g=gmlp_ln_g[l], gmlp_ln_b=gmlp_ln_b[l], gmlp_w_s=gmlp_w_s[l],
                 gmlp_b_s=gmlp_b_s[l], pool_w=pool_w[l], pool_scale=pool_scale[l],
                 w_br_ssm=w_br_ssm[l], w_br_gmlp=w_br_gmlp[l], w_br_pool=w_br_pool[l], w_o=w_o[l],
                 ln1_g=ln1_g[l], ln1_b=ln1_b[l], ffn_w_up=ffn_w_up[l], ffn_conv_w=ffn_conv_w[l],
                 ffn_conv_b=ffn_conv_b[l], ffn_w_down=ffn_w_down[l], ln2_g=ln2_g[l], ln2_b=ln2_b[l])
        hp, a_re_n, a_im_n, _, pool_n, conv_n = trunk_layer(hp, None, None, None, None, 0, p)
        p_re.append(a_re_n)
        p_im.append(a_im_n)
        p_pool.append(pool_n)
        p_conv.append(conv_n)
        hs, b_re_n, b_im_n, v_n, pool_m, conv_m = trunk_layer(
            hs, state_ssm_re[l], state_ssm_im[l], state_pool[l], state_ffn_conv[l], PAST_LEN, p)
        s_re.append(b_re_n)
        s_im.append(b_im_n)
        s_v.append(v_n)
        s_pool.append(pool_m)
        s_conv.append(conv_m)
    return (hp, hs,
            jnp.stack(p_re), jnp.stack(p_im), jnp.stack(p_pool), jnp.stack(p_conv),
            jnp.stack(s_re), jnp.stack(s_im), jnp.stack(s_v), jnp.stack(s_pool), jnp.stack(s_conv))
```

```python
import math
import os
import sys
from contextlib import ExitStack
import numpy as np
import concourse.bass as bass
import concourse.mybir as mybir
from concourse.bass_utils import run_bass_kernel_spmd

F32 = mybir.dt.float32
BF16 = mybir.dt.bfloat16
I32 = mybir.dt.int32
AF = mybir.ActivationFunctionType
ALU = mybir.AluOpType

L = 2
D = 1024
KC = 8
TP = 2048
NSM = 128
NT = TP + NSM
INC = 4608
O1, O2, O3, O4 = 512, 768, 1024, 1536
DFF = 2816
F2 = 5632
NPAIR = 22
ALPHA = float((2 * L) ** 0.25)
EPS = 1e-5
TILES = [(0, 512), (512, 512), (1024, 512), (1536, 512), (2048, 128)]
NB = 17
SP_D, SP_BGLU, SP_PSC, SP_BG, SP_LN1G, SP_LN1B, SP_LN2G, SP_LN2B = 0, 4, 8, 12, 36, 44, 52, 60
SP_CW, SP_CB, SP_ARE, SP_AIM, SP_LS = 68, 200, 244, 260, 276
NSP = 292
BC_LNG, BC_LNB, BC_BSP, BC_BSS = 0, 256, 512, 768
NBC = 1024


class Buf:
    def __init__(self, bid, space, lo, hi, ap, name):
        self.id, self.space, self.lo, self.hi, self.ap, self.name = bid, space, lo, hi, ap, name

    def k(self, sub=None):
        return (self.id, sub)

    def __getitem__(self, idx):
        return self.ap[idx]


class Prog:
    def __init__(self, nc, st, arena_bytes):
        self.nc, self.st = nc, st
        self.ops = []
        self.tags = []
        self.engs = {'pe': nc.tensor, 'act': nc.scalar, 'dve': nc.vector, 'pool': nc.gpsimd, 'sp': nc.sync}
        self.bufs = []
        self.alias = {}
        self.arena = st.enter_context(nc.sbuf_tensor("arena", [128, arena_bytes // 4], F32))
        self.arena_bytes = arena_bytes
        self.top = 0
        self.psum = st.enter_context(nc.psum_tensor("psa", [128, 8 * 512], F32))

    def _mk(self, space, lo, hi, ap, name):
        b = Buf(len(self.bufs), space, lo, hi, ap, name)
        al = set()
        for o in self.bufs:
            if o.space == space and o.lo < hi and lo < o.hi:
                al.add(o.id)
                self.alias[o.id].add(b.id)
        self.alias[b.id] = al
        self.bufs.append(b)
        return b

    def alloc(self, name, shape, dt=F32):
        esz = 2 if dt == BF16 else 4
        n = 1
        for s in shape[1:]:
            n *= s
        nbytes = (n * esz + 31) // 32 * 32
        lo = self.top
        hi = lo + nbytes
        assert hi <= self.arena_bytes, f"arena overflow at {name}: {hi}"
        self.top = hi
        ap = self.arena[:, lo // 4:(lo + n * esz + 3) // 4]
        if dt != F32:
            ap = ap.bitcast(dt)
            ap = ap[:, 0:n]
        if len(shape) > 2:
            names = "abcd"[:len(shape) - 1]
            pat = "p (" + " ".join(names) + ") -> p " + " ".join(names)
            ap = ap.rearrange(pat, **{nm: s for nm, s in zip(names, shape[1:])})
        if shape[0] < 128:
            ap = ap[0:shape[0]]
        return self._mk('sb', lo, hi, ap, name)

    def alloc_at(self, name, shape, dt, lo):
        top = self.top
        self.top = lo
        b = self.alloc(name, shape, dt)
        self.top = top
        return b

    def mark(self):
        return self.top

    def release(self, m):
        self.top = m

    def psum_buf(self, name, col0, ncols, shape=None):
        ap = self.psum[:, col0:col0 + ncols]
        if shape is not None and len(shape) > 2:
            names = "abcd"[:len(shape) - 1]
            pat = "p (" + " ".join(names) + ") -> p " + " ".join(names)
            ap = ap.rearrange(pat, **{nm: s for nm, s in zip(names, shape[1:])})
        return self._mk('ps', col0 * 4, (col0 + ncols) * 4, ap, name)

    def dram(self, name, shape, kind):
        t = self.nc.dram_tensor(name, list(shape), F32, kind=kind)
        return self._mk('dram_' + name, 0, 1, t.ap(), name)

    def op(self, eng, fn, r=(), w=(), dma=False, chan=None):
        f = sys._getframe(1)
        while f.f_code.co_name in ('dma', 'mm', 'act', 'tt', 'ts', 'stt', 'cp', 'memset', 'op'):
            f = f.f_back
        self.tags.append(f.f_lineno)
        self.ops.append((eng, fn, list(r), list(w), dma, chan))

    def dma(self, q, out, in_, r=(), w=(), chan=None):
        self.op(q, lambda e: e.dma_start(out=out, in_=in_), r, w, dma=True, chan=chan)

    def mm(self, out, lhsT, rhs, start, stop, r, w):
        self.op('pe', lambda e: e.matmul(out, lhsT, rhs, start=start, stop=stop), r, w)

    def act(self, out, in_, func, r, w, bias=None, scale=None):
        kw = {}
        if bias is not None:
            kw['bias'] = bias
        if scale is not None:
            kw['scale'] = scale
        self.op('act', lambda e: e.activation(out, in_, func, **kw), r, w)

    def tt(self, out, a, b, op, r, w, eng='dve'):
        self.op(eng, lambda e: e.tensor_tensor(out, a, b, op), r, w)

    def ts(self, out, a, s1, s2, op0, op1, r, w, eng='dve'):
        if op1 is None:
            self.op(eng, lambda e: e.tensor_scalar(out, a, s1, None, op0), r, w)
        else:
            self.op(eng, lambda e: e.tensor_scalar(out, a, s1, s2, op0, op1), r, w)

    def stt(self, out, a, s, b, op0, op1, r, w):
        self.op('dve', lambda e: e.scalar_tensor_tensor(out, a, s, b, op0, op1), r, w)

    def cp(self, out, in_, r, w, eng='dve'):
        if eng == 'act':
            self.op('act', lambda e: e.copy(out, in_), r, w)
        else:
            self.op(eng, lambda e: e.tensor_copy(out, in_), r, w)

    def memset(self, ap, val, w, eng='dve'):
        self.op(eng, lambda e: e.memset(ap, val), (), w)

    def _conf(self, state, key):
        bid, sub = key
        for (b2, s2), stt_ in state.get(bid, {}).items():
            if sub is None or s2 is None or s2 == sub:
                yield stt_
        for a in self.alias[bid]:
            for stt_ in state.get(a, {}).values():
                yield stt_

    def emit(self):
        ops = self.ops
        n = len(ops)
        state = {}
        deps = [None] * n
        for i, (eng, fn, rd, wr, isdma, chan) in enumerate(ops):
            d = set()
            for k in rd:
                for s_ in self._conf(state, k):
                    if s_[0] is not None:
                        d.add(s_[0])
            for k in wr:
                for s_ in self._conf(state, k):
                    if s_[0] is not None:
                        d.add(s_[0])
                    d.update(s_[1].values())
                    d.update(s_[2])
            d.discard(i)
            if eng == 'pe':
                d = {x for x in d if ops[x][0] != 'pe'}
            deps[i] = d
            for k in rd:
                s_ = state.setdefault(k[0], {}).setdefault(k, [None, {}, []])
                if isdma:
                    s_[2].append(i)
                else:
                    s_[1][eng] = i
            for k in wr:
                bid, sub = k
                for (b2, s2), s_ in state.get(bid, {}).items():
                    if sub is None or s2 is None or s2 == sub:
                        s_[0] = i
                        s_[1] = {}
                        s_[2] = []
                state.setdefault(bid, {})[k] = [i, {}, []]
        sig = set()
        for d in deps:
            sig.update(d)
        sems = {}

        def get_sem(name):
            if name not in sems:
                sems[name] = self.st.enter_context(self.nc.semaphore(name))
            return sems[name]
        eng_cnt, dma_cnt, signal, waited = {}, {}, {}, {}
        n_wait = 0
        for i, (eng, fn, rd, wr, isdma, chan) in enumerate(ops):
            eobj = self.engs[eng]
            wd = waited.setdefault(eng, {})
            need = {}
            for d in deps[i]:
                sn, v, isd = signal[d]
                if isd:
                    v = dma_cnt[sn]
                if wd.get(sn, 0) < v:
                    need[sn] = max(need.get(sn, 0), v)
            for sn, v in need.items():
                eobj.wait_ge(get_sem(sn), v)
                wd[sn] = v
                n_wait += 1
            inst = fn(eobj)
            if isdma:
                sn = 'd_' + str(chan)
                dma_cnt[sn] = dma_cnt.get(sn, 0) + 16
                inst.then_inc(get_sem(sn), 16)
                signal[i] = (sn, dma_cnt[sn], True)
            elif i in sig:
                sn = 'e_' + eng
                eng_cnt[sn] = eng_cnt.get(sn, 0) + 1
                inst.then_inc(get_sem(sn), 1)
                signal[i] = (sn, eng_cnt[sn], False)
        self.deps = deps
        self.signal = signal
        self.stats = dict(n_ops=n, n_wait=n_wait, n_sems=len(sems), eng_cnt=eng_cnt,
                          n_dma=sum(v for v in dma_cnt.values()) // 16)
        return self.stats


def build_program():
    nc = bass.Bass("TRN2", target_bir_lowering=False)
    st = ExitStack()
    with st:
        P = Prog(nc, st, 212480)
        _build(nc, P)
        stats = P.emit()
        _CACHE['P'] = P
    return nc, stats


def _build(nc, P):
    def din(name, shape):
        return P.dram(name, shape, "ExternalInput")

    def dout(name, shape):
        return P.dram(name, shape, "ExternalOutput")

    xT0 = din("xT0", [128, KC, NT])
    h0 = din("h0", [L, 128, 2, 16, 16])
    poolpast = din("poolpast", [L, 128, 4, 16, 15])
    convpast = din("convpast", [L, 128, 44, 16, 2])
    w_in = din("w_in", [L, 128, KC, INC])
    w_br = din("w_br", [L, 128, 10, D])
    w_o = din("w_o", [L, 128, KC, D])
    w_glu = din("w_glu", [L, 128, 4, 512])
    w_up = din("w_up", [L, 128, KC, F2])
    w_dn = din("w_dn", [L, 128, NPAIR, D])
    w_pool = din("w_pool", [L, 128, 4, 128])
    bpad = din("bpad", [L, 128, 2, 16, 128])
    cpad = din("cpad", [L, 128, 2, 16, 128])
    bpad2 = din("bpad2", [L, 128, 2, 16, 128])
    wts = din("wts", [L, 128, 2, 4, 128])
    smallp = din("smallp", [L, 128, NSP])
    bcp = din("bcp", [L, 128, NBC])
    rcc = din("rcc", [128, 4, 16])

    yT = dout("yT", [128, KC, NT])
    o_ssm_p = dout("o_ssm_p", [L, 128, 2, 16])
    o_ssm_s = dout("o_ssm_s", [L, 128, 2, 16, 16])
    o_pool_p = dout("o_pool_p", [L, 128, 4, 15])
    o_pool_s = dout("o_pool_s", [L, 128, 4, 16, 15])
    o_conv_p = dout("o_conv_p", [L, 128, 44, 2])
    o_conv_s = dout("o_conv_s", [L, 128, 44, 16, 2])
    o_v_s = dout("o_v_s", [L, 128, 256])
    res = [P.dram(f"res{i}", [128, KC, NT], "Internal") for i in range(3)]
    outs = [yT, o_ssm_p, o_ssm_s, o_pool_p, o_pool_s, o_conv_p, o_conv_s, o_v_s]

    PS = [P.psum_buf(f"ps{i}", i * 512, 512) for i in range(8)]

    def bank(i):
        return PS[i]

    xb = P.alloc("xb", [128, KC, NT], BF16)
    ones_bf = P.alloc("ones_bf", [128, 128], BF16)
    ones_f = P.alloc("ones_f", [128, 128], F32)
    ident = P.alloc("ident", [128, 128], F32)
    tril = P.alloc("tril", [128, 128], F32)
    iot = P.alloc("iot", [128, 128], F32)
    rc = P.alloc("rc", [128, 4, 16], F32)
    spm = [P.alloc(f"spm{l}", [128, NSP], F32) for l in range(L)]
    base_mark = P.mark()

    def tile_of(c0):
        return min(c0 // 512, 4)

    iot_i = P.alloc("iot_i", [128, 128], I32)
    P.memset(ones_bf[:], 1.0, [ones_bf.k()])
    P.memset(ones_f[:], 1.0, [ones_f.k()])
    P.op('pool', lambda e: e.affine_select(tril[:], ones_f[:], [[1, 128]], ALU.is_ge, 0.0, base=0,
                                           channel_multiplier=-1), [ones_f.k()], [tril.k()])
    P.op('pool', lambda e: e.affine_select(ident[:], tril[:], [[-1, 128]], ALU.is_ge, 0.0, base=0,
                                           channel_multiplier=1), [tril.k()], [ident.k()])
    P.op('pool', lambda e: e.iota(iot_i[:], [[1, 128]], base=0, channel_multiplier=0), [], [iot_i.k()])
    P.cp(iot[:], iot_i[:], [iot_i.k()], [iot.k()])
    P.dma('sp', rc[:], rcc.ap, [rcc.k()], [rc.k()], chan='rc')
    for l in range(L):
        P.dma('sp', spm[l][:], smallp.ap[l], [smallp.k()], [spm[l].k()], chan=f'spm{l}')
    for ti, (c0, n) in enumerate(TILES):
        P.dma('pool', xb[:, :, c0:c0 + n], xT0.ap[:, :, c0:c0 + n], [xT0.k()], [xb.k(ti)], chan='xb')
    P.release(base_mark)

    def sincos(ang, cs, sn, scr, n, keys_r):
        (s0, k0), (s1, k1), (s2, k2), (s3, k3), (si, ki) = scr
        P.ts(s0, ang, 1.0 / (2 * math.pi), None, ALU.mult, None, keys_r, [k0])
        P.cp(si, s0, [k0], [ki])
        P.cp(s0, si, [ki], [k0])
        P.stt(s1, s0, -2.0 * math.pi, ang, ALU.mult, ALU.add, [k0] + keys_r, [k1])
        P.act(s2, s1, AF.Sin, [k1], [k2], scale=0.5)
        P.act(s3, s1, AF.Sin, [k1], [k3], scale=0.25)
        P.tt(s0, s3, s3, ALU.mult, [k3], [k0])
        P.ts(s0, s0, -2.0, 1.0, ALU.mult, ALU.add, [k0], [k0])
        return (s0, k0), (s1, k1), (s2, k2), (s3, k3)

    def sincos_finish(cs, sn, kcs, ksn, s0, k0, s2, k2, s3, k3):
        P.stt(sn, s2, 2.0, s0, ALU.mult, ALU.mult, [k2, k0], [ksn])
        P.tt(s3, s2, s2, ALU.mult, [k2], [k3])
        P.ts(cs, s3, -2.0, 1.0, ALU.mult, ALU.add, [k3], [kcs])

    def resid_ln(l, which, ti, psum_dc, src, dst, goff, boff, extra_r, ln_m):
        c0, n = TILES[ti]
        xt = ln_m['xt'][ti % 2]
        P.dma('sp', xt[:, :, 0:n], src.ap[:, :, c0:c0 + n], [src.k(ti)], [xt.k()], chan=f'xt{ti % 2}')
        for dc in range(KC):
            pap, pk = psum_dc(dc)
            P.stt(xt[:, dc, 0:n], xt[:, dc, 0:n], ALPHA, pap, ALU.mult, ALU.add, [xt.k(), pk], [xt.k()])
        xbt, sqt = ln_m['xbt'], ln_m['sqt']
        P.cp(xbt[:, :, 0:n], xt[:, :, 0:n], [xt.k()], [xbt.k()], eng='act')
        P.act(sqt[:, :, 0:n], xt[:, :, 0:n], AF.Square, [xt.k()], [sqt.k()])
        s1, s2 = bank(6), bank(7)
        for dc in range(KC):
            P.mm(s1[:, 0:n], ones_bf[:], xbt[:, dc, 0:n], dc == 0, dc == KC - 1, [ones_bf.k(), xbt.k()], [s1.k()])
        for dc in range(KC):
            P.mm(s2[:, 0:n], ones_bf[:], sqt[:, dc, 0:n], dc == 0, dc == KC - 1, [ones_bf.k(), sqt.k()], [s2.k()])
        mean, var, rstd = ln_m['mean'], ln_m['var'], ln_m['rstd']
        P.op('act', lambda e: e.mul(mean[:, 0:n], s1[:, 0:n], 1.0 / D), [s1.k()], [mean.k()])
        P.tt(var[:, 0:n], mean[:, 0:n], mean[:, 0:n], ALU.mult, [mean.k()], [var.k()])
        P.stt(var[:, 0:n], s2[:, 0:n], 1.0 / D, var[:, 0:n], ALU.mult, ALU.subtract, [s2.k(), var.k()], [var.k()])
        P.ts(var[:, 0:n], var[:, 0:n], EPS, None, ALU.add, None, [var.k()], [var.k()])
        P.act(var[:, 0:n], var[:, 0:n], AF.Sqrt, [var.k()], [var.k()])
        P.op('dve', lambda e: e.reciprocal(rstd[:, 0:n], var[:, 0:n]), [var.k()], [rstd.k()])
        for dc in range(KC):
            P.tt(xt[:, dc, 0:n], xt[:, dc, 0:n], mean[:, 0:n], ALU.subtract, [xt.k(), mean.k()], [xt.k()])
            P.tt(xt[:, dc, 0:n], xt[:, dc, 0:n], rstd[:, 0:n], ALU.mult, [xt.k(), rstd.k()], [xt.k()])
            P.act(xt[:, dc, 0:n], xt[:, dc, 0:n], AF.Identity, [xt.k(), spm[l].k()], [xt.k()],
                  bias=spm[l][:, boff + dc:boff + dc + 1], scale=spm[l][:, goff + dc:goff + dc + 1])
        P.cp(xb[:, :, c0:c0 + n], xt[:, :, 0:n], [xt.k()] + extra_r, [xb.k(ti)], eng='act')
        P.dma('sp', dst.ap[:, :, c0:c0 + n], xt[:, :, 0:n], [xt.k()], [dst.k(ti)], chan=f'xt{ti % 2}')

    def resid_ln_seq(l, tis, mmfn, src, dst, goff, boff, ln_m, mode, xb_mode):
        bctr = [0]
        pend = {}
        xbt, sqt = ln_m['xbt'], ln_m['sqt']
        mean, var, rstd = ln_m['mean'], ln_m['var'], ln_m['rstd']
        s1, s2 = bank(6), bank(7)

        def A1(ti, dc):
            pb = bank(bctr[0] % 6)
            bctr[0] += 1
            mmfn(ti, dc, pb)
            return pb

        def pre(ti, dcs):
            for dc in dcs:
                pend[(ti, dc)] = A1(ti, dc)

        def XT(ti):
            return ln_m['xt'][ti % 2]

        def stepA(ti):
            c0, n = TILES[ti]
            xt = XT(ti)
            P.dma('sp', xt[:, :, 0:n], src.ap[:, :, c0:c0 + n], [src.k(ti)], [xt.k()], chan=f'xt{ti % 2}')
            for dc in range(KC):
                pb = pend.pop((ti, dc)) if (ti, dc) in pend else A1(ti, dc)
                P.stt(xt[:, dc, 0:n], xt[:, dc, 0:n], ALPHA, pb[:, 0:n], ALU.mult, ALU.add, [xt.k(dc), pb.k()], [xt.k(dc)])
            P.cp(xbt[:, :, 0:n], xt[:, :, 0:n], [xt.k()], [xbt.k()], eng='act')
            P.act(sqt[:, :, 0:n], xt[:, :, 0:n], AF.Square, [xt.k()], [sqt.k()])

        def stats_mm(ti):
            c0, n = TILES[ti]
            for dc in range(KC):
                P.mm(s1[:, 0:n], ones_bf[:], xbt[:, dc, 0:n], dc == 0, dc == KC - 1, [ones_bf.k(), xbt.k()], [s1.k()])
            for dc in range(KC):
                P.mm(s2[:, 0:n], ones_bf[:], sqt[:, dc, 0:n], dc == 0, dc == KC - 1, [ones_bf.k(), sqt.k()], [s2.k()])

        def stats_ew(ti):
            c0, n = TILES[ti]
            P.op('act', lambda e, n=n: e.mul(mean[:, 0:n], s1[:, 0:n], 1.0 / D), [s1.k()], [mean.k()])
            P.tt(var[:, 0:n], mean[:, 0:n], mean[:, 0:n], ALU.mult, [mean.k()], [var.k()])
            P.stt(var[:, 0:n], s2[:, 0:n], 1.0 / D, var[:, 0:n], ALU.mult, ALU.subtract, [s2.k(), var.k()], [var.k()])
            P.ts(var[:, 0:n], var[:, 0:n], EPS, None, ALU.add, None, [var.k()], [var.k()])
            P.act(var[:, 0:n], var[:, 0:n], AF.Sqrt, [var.k()], [var.k()])
            P.op('dve', lambda e, n=n: e.reciprocal(rstd[:, 0:n], var[:, 0:n]), [var.k()], [rstd.k()])

        def norm(ti):
            c0, n = TILES[ti]
            xt = XT(ti)
            for dc in range(KC):
                P.tt(xt[:, dc, 0:n], xt[:, dc, 0:n], mean[:, 0:n], ALU.subtract, [xt.k(dc), mean.k()], [xt.k(dc)])
                P.tt(xt[:, dc, 0:n], xt[:, dc, 0:n], rstd[:, 0:n], ALU.mult, [xt.k(dc), rstd.k()], [xt.k(dc)])
                P.act(xt[:, dc, 0:n], xt[:, dc, 0:n], AF.Identity, [xt.k(dc), spm[l].k()], [xt.k(dc)],
                      bias=spm[l][:, boff + dc:boff + dc + 1], scale=spm[l][:, goff + dc:goff + dc + 1])
            if xb_mode == 'act':
                P.cp(xb[:, :, c0:c0 + n], xt[:, :, 0:n], [xt.k()], [xb.k(ti)], eng='act')
            P.dma('sp', dst.ap[:, :, c0:c0 + n], xt[:, :, 0:n], [xt.k()], [dst.k(ti)], chan=f'xt{ti % 2}')
            if xb_mode == 'dma':
                P.dma('pool', xb[:, :, c0:c0 + n], dst.ap[:, :, c0:c0 + n], [dst.k(ti)], [xb.k(ti)], chan='xb')

        nt = len(tis)
        if mode == 'p4':
            pre(tis[0], range(6))
            for i, ti in enumerate(tis):
                stepA(ti)
                if i + 1 < nt:
                    pre(tis[i + 1], range(2))
                stats_mm(ti)
                if i + 1 < nt:
                    pre(tis[i + 1], range(2, 6))
                if i > 0:
                    norm(tis[i - 1])
                stats_ew(ti)
            last_ti = tis[-1]
            return lambda: norm(last_ti)
        else:
            pre(tis[0], range(6))
            for i, ti in enumerate(tis):
                stepA(ti)
                if i + 1 < nt:
                    pre(tis[i + 1], range(2))
                stats_mm(ti)
                if i + 1 < nt:
                    pre(tis[i + 1], range(2, 6))
                stats_ew(ti)
                norm(ti)

    KSTOP = int(os.environ.get('KSTOP', '99'))
    stage = [0]

    def stop_here():
        stage[0] += 1
        return stage[0] >= KSTOP
    for l in range(L):
        if stage[0] >= KSTOP:
            break
        src0 = xT0 if l == 0 else res[1]
        mid = res[0] if l == 0 else res[2]
        dst2 = res[1] if l == 0 else yT
        sp = spm[l]

        ya = P.alloc("ya", [128, 4, NT], BF16)
        m_ssm = P.mark()
        win_s = P.alloc("win_s", [128, KC, 512], BF16)
        wglu = P.alloc("wglu", [128, 4, 512], BF16)
        lb1 = P.alloc("lhsT_B1", [128, 2, 16, 128], BF16)
        lb0 = P.alloc("lhsT_B0", [128, 2, 16, 128], BF16)
        lc = P.alloc("lhsT_C", [128, 2, 16, 128], BF16)
        lg = P.alloc("lhsT_G", [128, 2, 16, 128], BF16)
        k0m = P.alloc("k0m", [128, 4, 128], BF16)
        Ec = P.alloc("Ec", [128, 16, 64], F32)
        Es = P.alloc("Es", [128, 16, 64], F32)
        DkP = P.alloc("DkP", [128, 16, 64], F32)
        DkS = P.alloc("DkS", [128, 16, 64], F32)
        pw = P.alloc("pw", [128, 32, 16], F32)
        pwi = P.alloc("pwi", [128, 16], I32)
        hp = P.alloc("hp", [128, 2, 16], F32)
        h0t = P.alloc("h0t", [128, 2, 16, 16], F32)
        hso = P.alloc("hso", [128, 2, 16, 16], F32)
        sc = [P.alloc(f"sc{i}", [128, 1024], F32) for i in range(8)]
        sci = P.alloc("sci", [128, 1024], I32)
        P.dma('pool', win_s[:], w_in.ap[l, :, :, 0:O1], [w_in.k()], [win_s.k()], chan='win_s')
        P.dma('pool', wglu[:], w_glu.ap[l], [w_glu.k()], [wglu.k()], chan='wglu')
        P.dma('sp', h0t[:], h0.ap[l], [h0.k()], [h0t.k()], chan='h0t')

        def S(i):
            return pw[:, i, :], pw.k(i)
        are = sp[:, SP_ARE:SP_ARE + 16]
        aim = sp[:, SP_AIM:SP_AIM + 16]
        lsp = sp[:, SP_LS:SP_LS + 16]
        (dt_, kdt), (x1, kx1), (mag, kmag), (ang, kang) = S(0), S(1), S(2), S(3)
        P.act(dt_, lsp, AF.Exp, [sp.k()], [kdt])
        P.tt(x1, are, dt_, ALU.mult, [sp.k(), kdt], [kx1])
        P.act(mag, x1, AF.Exp, [kx1], [kmag])
        P.tt(ang, aim, dt_, ALU.mult, [sp.k(), kdt], [kang])
        r0, r1, r2, r3 = sincos(ang, None, None, [S(4), S(5), S(6), S(7), (pwi[:], pwi.k())], 16, [kang])
        (cs, kcs), (sn, ksn) = S(8), S(9)
        angp, kangp = r1
        sincos_finish(cs, sn, kcs, ksn, r0[0], r0[1], r2[0], r2[1], r3[0], r3[1])
        (abr, kabr), (abi, kabi) = S(10), S(11)
        P.tt(abr, mag, cs, ALU.mult, [kmag, kcs], [kabr])
        P.tt(abi, mag, sn, ALU.mult, [kmag, ksn], [kabi])
        (den, kden), (tq, ktq), (nr, knr), (kr, kkr), (ki_, kki) = S(12), S(13), S(14), S(15), S(16)
        (ta, kta), (tb, ktb) = S(17), S(18)
        P.tt(den, are, are, ALU.mult, [sp.k()], [kden])
        P.tt(tq, aim, aim, ALU.mult, [sp.k()], [ktq])
        P.tt(den, den, tq, ALU.add, [kden, ktq], [kden])
        P.op('dve', lambda e: e.reciprocal(den, den), [kden], [kden])
        P.ts(nr, abr, -1.0, None, ALU.add, None, [kabr], [knr])
        P.tt(ta, nr, are, ALU.mult, [knr, sp.k()], [kta])
        P.tt(tb, abi, aim, ALU.mult, [kabi, sp.k()], [ktb])
        P.tt(ta, ta, tb, ALU.add, [kta, ktb], [kta])
        P.tt(kr, ta, den, ALU.mult, [kta, kden], [kkr])
        P.tt(ta, abi, are, ALU.mult, [kabi, sp.k()], [kta])
        P.tt(tb, nr, aim, ALU.mult, [knr, sp.k()], [ktb])
        P.tt(ta, ta, tb, ALU.subtract, [kta, ktb], [kta])
        P.tt(ki_, ta, den, ALU.mult, [kta, kden], [kki])
        (akr, kakr), (aki, kaki), (a2r, ka2r), (a2i, ka2i), (mag2, kmag2), (ang2, kang2) = S(19), S(20), S(21), S(22), S(23), S(24)
        P.tt(ta, abr, kr, ALU.mult, [kabr, kkr], [kta])
        P.tt(tb, abi, ki_, ALU.mult, [kabi, kki], [ktb])
        P.tt(akr, ta, tb, ALU.subtract, [kta, ktb], [kakr])
        P.tt(ta, abr, ki_, ALU.mult, [kabr, kki], [kta])
        P.tt(tb, abi, kr, ALU.mult, [kabi, kkr], [ktb])
        P.tt(aki, ta, tb, ALU.add, [kta, ktb], [kaki])
        P.tt(ta, abr, abr, ALU.mult, [kabr], [kta])
        P.tt(tb, abi, abi, ALU.mult, [kabi], [ktb])
        P.tt(a2r, ta, tb, ALU.subtract, [kta, ktb], [ka2r])
        P.stt(a2i, abr, 2.0, abi, ALU.mult, ALU.mult, [kabr, kabi], [ka2i])
        P.tt(mag2, mag, mag, ALU.mult, [kmag], [kmag2])
        P.ts(ang2, angp, 2.0, None, ALU.mult, None, [kangp], [kang2])
        for hfs in range(2):
            H0 = hfs * 8
            bre, bim = sc[3], sc[4]
            P.dma('sp', bre[:], bpad.ap[l, :, 0, H0:H0 + 8].rearrange("p a b -> p (a b)"), [bpad.k()], [bre.k()], chan='bre')
            P.dma('sp', bim[:], bpad.ap[l, :, 1, H0:H0 + 8].rearrange("p a b -> p (a b)"), [bpad.k()], [bim.k()], chan='bim')
            for (ks_r, kk_r, ks_i, kk_i, lbx) in ((kr, kkr, ki_, kki, lb1), (akr, kakr, aki, kaki, lb0)):
                krF, kiF = sc[0], sc[1]
                for (ksrc, kk, dstF) in ((ks_r, kk_r, krF), (ks_i, kk_i, kiF)):
                    dg = sc[2]
                    dg3 = dg[:].rearrange("p (a b) -> p a b", a=8)
                    P.tt(dg3, ident[:].unsqueeze(1).to_broadcast([128, 8, 128]),
                         ksrc[:, H0:H0 + 8].unsqueeze(2).to_broadcast([128, 8, 128]), ALU.mult, [ident.k(), kk], [dg.k()])
                    for j in range(2):
                        P.mm(bank(j)[:], ones_f[:], dg[:, j * 512:(j + 1) * 512], True, True, [ones_f.k(), dg.k()], [bank(j).k()])
                        P.cp(dstF[:, j * 512:(j + 1) * 512], bank(j)[:], [bank(j).k()], [dstF.k()], eng='act')
                t5, t6 = sc[5], sc[6]
                lb_re = lbx[:, 0, H0:H0 + 8].rearrange("p a b -> p (a b)")
                lb_im = lbx[:, 1, H0:H0 + 8].rearrange("p a b -> p (a b)")
                P.tt(t5[:], krF[:], bre[:], ALU.mult, [krF.k(), bre.k()], [t5.k()])
                P.tt(t6[:], kiF[:], bim[:], ALU.mult, [kiF.k(), bim.k()], [t6.k()])
                P.tt(lb_re, t5[:], t6[:], ALU.subtract, [t5.k(), t6.k()], [lbx.k((0, hfs))])
                P.tt(t5[:], krF[:], bim[:], ALU.mult, [krF.k(), bim.k()], [t5.k()])
                P.tt(t6[:], kiF[:], bre[:], ALU.mult, [kiF.k(), bre.k()], [t6.k()])
                P.tt(lb_im, t5[:], t6[:], ALU.add, [t5.k(), t6.k()], [lbx.k((1, hfs))])
            cre, cim = sc[3], sc[4]
            P.dma('sp', cre[:], cpad.ap[l, :, 0, H0:H0 + 8].rearrange("p a b -> p (a b)"), [cpad.k()], [cre.k()], chan='bre')
            P.dma('sp', cim[:], cpad.ap[l, :, 1, H0:H0 + 8].rearrange("p a b -> p (a b)"), [cpad.k()], [cim.k()], chan='bim')
            P.cp(lc[:, 0, H0:H0 + 8].rearrange("p a b -> p (a b)"), cre[:], [cre.k()], [lc.k((0, hfs))], eng='act')
            P.op('act', lambda e, H0=H0, cim=cim: e.mul(lc[:, 1, H0:H0 + 8].rearrange("p a b -> p (a b)"), cim[:], -1.0),
                 [cim.k()], [lc.k((1, hfs))])
            arb = abr[:, H0:H0 + 8].unsqueeze(2).to_broadcast([128, 8, 128])
            aib = abi[:, H0:H0 + 8].unsqueeze(2).to_broadcast([128, 8, 128])
            t5, t6 = sc[5], sc[6]
            t53 = t5[:].rearrange("p (a b) -> p a b", a=8)
            t63 = t6[:].rearrange("p (a b) -> p a b", a=8)
            cre3 = cre[:].rearrange("p (a b) -> p a b", a=8)
            cim3 = cim[:].rearrange("p (a b) -> p a b", a=8)
            P.tt(t53, cre3, arb, ALU.mult, [cre.k(), kabr], [t5.k()])
            P.tt(t63, cim3, aib, ALU.mult, [cim.k(), kabi], [t6.k()])
            P.tt(lg[:, 0, H0:H0 + 8].rearrange("p a b -> p (a b)"), t5[:], t6[:], ALU.subtract, [t5.k(), t6.k()], [lg.k((0, hfs))])
            P.tt(t53, cre3, aib, ALU.mult, [cre.k(), kabi], [t5.k()])
            P.tt(t63, cim3, arb, ALU.mult, [cim.k(), kabr], [t6.k()])
            P.stt(lg[:, 1, H0:H0 + 8].rearrange("p a b -> p (a b)"), t5[:], -1.0, t6[:], ALU.mult, ALU.subtract,
                  [t5.k(), t6.k()], [lg.k((1, hfs))])
            b2r, b2i = sc[0], sc[1]
            P.dma('sp', b2r[:], bpad2.ap[l, :, 0, H0:H0 + 8].rearrange("p a b -> p (a b)"), [bpad2.k()], [b2r.k()], chan='b2r')
            P.dma('sp', b2i[:], bpad2.ap[l, :, 1, H0:H0 + 8].rearrange("p a b -> p (a b)"), [bpad2.k()], [b2i.k()], chan='b2i')
            krb = kr[:, H0:H0 + 8].unsqueeze(2).to_broadcast([128, 8, 128])
            kib = ki_[:, H0:H0 + 8].unsqueeze(2).to_broadcast([128, 8, 128])
            b2r3 = b2r[:].rearrange("p (a b) -> p a b", a=8)
            b2i3 = b2i[:].rearrange("p (a b) -> p a b", a=8)
            xr, xi = sc[2], sc[7]
            xrb = xr[:].bitcast(BF16)[:, 0:1024].rearrange("p (a b) -> p a b", a=8)
            xib = xi[:].bitcast(BF16)[:, 0:1024].rearrange("p (a b) -> p a b", a=8)
            P.tt(t53, b2r3, krb, ALU.mult, [b2r.k(), kkr], [t5.k()])
            P.tt(t63, b2i3, kib, ALU.mult, [b2i.k(), kki], [t6.k()])
            P.tt(xrb, t53, t63, ALU.subtract, [t5.k(), t6.k()], [xr.k()])
            P.tt(t53, b2i3, krb, ALU.mult, [b2i.k(), kkr], [t5.k()])
            P.tt(t63, b2r3, kib, ALU.mult, [b2r.k(), kki], [t6.k()])
            P.tt(xib, t53, t63, ALU.add, [t5.k(), t6.k()], [xi.k()])
            for cq in range(2):
                cc = 2 * hfs + cq
                pk = bank(2 + cq)
                idx = 0
                for gq in range(4):
                    gp = cc * 4 + gq
                    gl = gp - H0
                    for (xb_, ri) in ((xrb, 0), (xib, 1)):
                        P.mm(pk[:, 0:128], xb_[:, gl, :], lc[:, ri, gp, :], idx == 0, idx == 7,
                             [xr.k(), xi.k(), lc.k()], [pk.k()])
                        idx += 1
                P.cp(k0m[:, cc, :], pk[:, 0:128], [pk.k()], [k0m.k(cc)], eng='act')
            angt = sc[3]
            angt3 = angt[:, 0:512].rearrange("p (a b) -> p a b", a=8)
            P.tt(angt3, ang2[:, H0:H0 + 8].unsqueeze(2).to_broadcast([128, 8, 64]),
                 iot[:, 0:64].unsqueeze(1).to_broadcast([128, 8, 64]), ALU.mult, [kang2, iot.k()], [angt.k()])
            q0, q1, q2, q3 = sincos(angt[:, 0:512], None, None,
                                    [(sc[4][:, 0:512], sc[4].k()), (sc[5][:, 0:512], sc[5].k()), (sc[6][:, 0:512], sc[6].k()),
                                     (sc[0][:, 0:512], sc[0].k()), (sci[:, 0:512], sci.k())], 512, [angt.k()])
            sincos_finish(Ec[:, H0:H0 + 8].rearrange("p a b -> p (a b)"), Es[:, H0:H0 + 8].rearrange("p a b -> p (a b)"),
                          Ec.k(hfs), Es.k(hfs), q0[0], q0[1], q2[0], q2[1], q3[0], q3[1])
        P.cp(DkP[:], mag2.unsqueeze(2).to_broadcast([128, 16, 64]), [kmag2], [DkP.k()])
        P.memset(DkP[:, :, 0:1], 0.0, [DkP.k()])
        P.cp(DkS[:], mag2.unsqueeze(2).to_broadcast([128, 16, 64]), [kmag2], [DkS.k()])
        P.memset(DkS[:].rearrange("p a (s t) -> p a s t", t=4)[:, :, :, 0:1], 0.0, [DkS.k()])

        pu, py, pg_ = bank(0), bank(5), bank(6)
        pbr = P.psum_buf("pbr", 512, 512, [128, 8, 64])
        pbi = P.psum_buf("pbi", 1536, 512, [128, 8, 64])
        ubl = [P.alloc(f"ub{i}", [128, 4, 2, 64], BF16) for i in range(2)]
        ufl = [P.alloc(f"uf{i}", [128, 4, 2, 64], F32) for i in range(2)]
        brl = [P.alloc(f"br{i}", [128, 8, 64], F32) for i in range(2)]
        bil = [P.alloc(f"bi{i}", [128, 8, 64], F32) for i in range(2)]
        hb = P.alloc("hb", [128, 2, 8, 80], BF16)
        zf = P.alloc("zf", [128, 4, 128], F32)
        zb = P.alloc("zb", [128, 4, 128], BF16)
        sg = P.alloc("sg", [128, 4, 128], F32)
        cw = P.alloc("cw", [128, 4, 8, 16], F32)
        t1, t2, t3, t4, btr, bti, gr, gi = [s_[:, 0:512].rearrange("p (a b) -> p a b", a=8) for s_ in sc]
        kt1, kt2, kt3, kt4, kbtr, kbti, kgr, kgi = [s_.k() for s_ in sc]

        def blk(b):
            c0 = b * 128
            return c0, (b == NB - 1), tile_of(c0)

        def views(b, hf):
            c0, smp, tix = blk(b)
            G0 = hf * 8
            ec, es = Ec[:, G0:G0 + 8, :], Es[:, G0:G0 + 8, :]
            if smp:
                ec = Ec[:, G0:G0 + 8, 0:4].unsqueeze(2).to_broadcast([128, 8, 16, 4])
                es = Es[:, G0:G0 + 8, 0:4].unsqueeze(2).to_broadcast([128, 8, 16, 4])

                def V(a):
                    return a.rearrange("p a (s t) -> p a s t", t=4)
            else:
                def V(a):
                    return a
            return G0, ec, es, V

        def stage_U(b):
            c0, smp, tix = blk(b)
            ub, uf = ubl[b % 2], ufl[b % 2]
            for cc in range(4):
                for k in range(KC):
                    P.mm(pu[:, cc * 128:(cc + 1) * 128], win_s[:, k, cc * 128:(cc + 1) * 128], xb[:, k, c0:c0 + 128],
                         k == 0, k == KC - 1, [win_s.k(), xb.k(tix)], [pu.k()])
            pu4 = pu[:].rearrange("p (c j e) -> p c e j", c=4, e=2)
            P.cp(ub[:], pu4, [pu.k()], [ub.k()], eng='act')
            P.cp(uf[:], pu4, [pu.k()], [uf.k()], eng='act')

        def stage_BU(b, hf):
            ub = ubl[b % 2]
            G0 = hf * 8
            s_ = (2 * b + hf) % 2
            for gl in range(8):
                gp = G0 + gl
                for (pbx, ri) in ((pbr, 0), (pbi, 1)):
                    P.mm(pbx[:, gl, :], lb0[:, ri, gp, :], ub[:, gp // 4, 0, :], True, False, [lb0.k(), ub.k()], [pbx.k()])
                    P.mm(pbx[:, gl, :], lb1[:, ri, gp, :], ub[:, gp // 4, 1, :], False, True, [lb1.k(), ub.k()], [pbx.k()])
            P.cp(brl[s_][:], pbr[:], [pbr.k()], [brl[s_].k()], eng='act')
            P.cp(bil[s_][:], pbi[:], [pbi.k()], [bil[s_].k()], eng='act')

        def stage_ROT(b, hf):
            c0, smp, tix = blk(b)
            G0, ec, es, V = views(b, hf)
            s_ = (2 * b + hf) % 2
            br_, bi_ = brl[s_], bil[s_]
            P.tt(V(t1), V(br_[:]), ec, ALU.mult, [br_.k(), Ec.k()], [kt1])
            P.tt(V(t2), V(bi_[:]), es, ALU.mult, [bi_.k(), Es.k()], [kt2])
            P.tt(V(t3), V(bi_[:]), ec, ALU.mult, [bi_.k(), Ec.k()], [kt3])
            P.tt(V(t4), V(br_[:]), es, ALU.mult, [br_.k(), Es.k()], [kt4])
            P.tt(btr, t1, t2, ALU.add, [kt1, kt2], [kbtr])
            P.tt(bti, t3, t4, ALU.subtract, [kt3, kt4], [kbti])
            if smp:
                hr, hi_ = h0t[:, 0, G0:G0 + 8, :], h0t[:, 1, G0:G0 + 8, :]
                khp = h0t.k()
                hb5 = hb[:].rearrange("p r a (s t) -> p r a s t", t=5)
                P.cp(hb5[:, :, :, :, 0], h0t[:, :, G0:G0 + 8, :], [khp], [hb.k('c')])
            elif b > 0:
                hr, hi_ = hp[:, 0, G0:G0 + 8], hp[:, 1, G0:G0 + 8]
                khp = hp.k(hf)
                P.cp(hb[:, :, :, 0], hp[:, :, G0:G0 + 8], [khp], [hb.k('c')])
            else:
                P.memset(hb[:, :, :, 0:1], 0.0, [hb.k('c')])
            if smp or b > 0:
                if smp:
                    ar = a2r[:, G0:G0 + 8].unsqueeze(2).to_broadcast([128, 8, 16])
                    ai = a2i[:, G0:G0 + 8].unsqueeze(2).to_broadcast([128, 8, 16])
                    c1, c2, c3, c4 = cw[:, 0], cw[:, 1], cw[:, 2], cw[:, 3]
                    b0r = btr.rearrange("p a (s t) -> p a s t", t=4)[:, :, :, 0]
                    b0i = bti.rearrange("p a (s t) -> p a s t", t=4)[:, :, :, 0]
                else:
                    ar, ai = a2r[:, G0:G0 + 8], a2i[:, G0:G0 + 8]
                    c1, c2, c3, c4 = cw[:, 0, :, 0], cw[:, 1, :, 0], cw[:, 2, :, 0], cw[:, 3, :, 0]
                    b0r, b0i = btr[:, :, 0], bti[:, :, 0]
                P.tt(c1, ar, hr, ALU.mult, [ka2r, khp], [cw.k(0)])
                P.tt(c2, ai, hi_, ALU.mult, [ka2i, khp], [cw.k(1)])
                P.tt(c1, c1, c2, ALU.subtract, [cw.k(0), cw.k(1)], [cw.k(0)])
                P.tt(b0r, b0r, c1, ALU.add, [kbtr, cw.k(0)], [kbtr])
                P.tt(c3, ar, hi_, ALU.mult, [ka2r, khp], [cw.k(2)])
                P.tt(c4, ai, hr, ALU.mult, [ka2i, khp], [cw.k(3)])
                P.tt(c3, c3, c4, ALU.add, [cw.k(2), cw.k(3)], [cw.k(2)])
                P.tt(b0i, b0i, c3, ALU.add, [kbti, cw.k(2)], [kbti])
            Dk = DkS if smp else DkP
            dk = Dk[:, G0:G0 + 8, :].rearrange("p a b -> p (a b)")
            P.op('dve', lambda e, dk=dk: e.tensor_tensor_scan(sc[6][:, 0:512], dk, sc[4][:, 0:512], 0.0, ALU.mult, ALU.add),
                 [Dk.k(), kbtr], [kgr])
            P.op('dve', lambda e, dk=dk: e.tensor_tensor_scan(sc[7][:, 0:512], dk, sc[5][:, 0:512], 0.0, ALU.mult, ALU.add),
                 [Dk.k(), kbti], [kgi])

        def stage_UNROT(b, hf):
            c0, smp, tix = blk(b)
            G0, ec, es, V = views(b, hf)
            P.tt(V(t1), V(gr), ec, ALU.mult, [kgr, Ec.k()], [kt1])
            P.tt(V(t2), V(gi), es, ALU.mult, [kgi, Es.k()], [kt2])
            P.tt(V(t3), V(gr), es, ALU.mult, [kgr, Es.k()], [kt3])
            P.tt(V(t4), V(gi), ec, ALU.mult, [kgi, Ec.k()], [kt4])
            if smp:
                hb5 = hb[:].rearrange("p r a (s t) -> p r a s t", t=5)
                P.tt(hb5[:, 0, :, :, 1:5], V(t1), V(t2), ALU.subtract, [kt1, kt2], [hb.k('d0')])
                P.tt(hb5[:, 1, :, :, 1:5], V(t3), V(t4), ALU.add, [kt3, kt4], [hb.k('d1')])
                P.tt(hso[:, 0, G0:G0 + 8, :], V(t1)[:, :, :, 3], V(t2)[:, :, :, 3], ALU.subtract, [kt1, kt2], [hso.k((0, hf))])
                P.tt(hso[:, 1, G0:G0 + 8, :], V(t3)[:, :, :, 3], V(t4)[:, :, :, 3], ALU.add, [kt3, kt4], [hso.k((1, hf))])
            else:
                P.tt(hb[:, 0, :, 1:65], t1, t2, ALU.subtract, [kt1, kt2], [hb.k('d0')])
                P.tt(hb[:, 1, :, 1:65], t3, t4, ALU.add, [kt3, kt4], [hb.k('d1')])
                P.tt(hp[:, 0, G0:G0 + 8], t1[:, :, 63], t2[:, :, 63], ALU.subtract, [kt1, kt2], [hp.k(hf)])
                P.tt(hp[:, 1, G0:G0 + 8], t3[:, :, 63], t4[:, :, 63], ALU.add, [kt3, kt4], [hp.k(hf)])

        def stage_CY(b, hf):
            c0, smp, tix = blk(b)
            ub = ubl[b % 2]
            G0 = hf * 8
            hb5 = hb[:].rearrange("p r a (s t) -> p r a s t", t=5)
            for cc in (2 * hf, 2 * hf + 1):
                for eo in range(2):
                    out = py[:, cc * 128 + eo * 64:cc * 128 + eo * 64 + 64]
                    idx = 0
                    nmm = 9 if eo == 0 else 8
                    for gq in range(4):
                        gp = cc * 4 + gq
                        gl = gp - G0
                        for ri in range(2):
                            if smp:
                                rhs = hb5[:, ri, gl, :, eo:eo + 4]
                            else:
                                rhs = hb[:, ri, gl, eo:eo + 64]
                            lh = lg if eo == 0 else lc
                            P.mm(out, lh[:, ri, gp, :], rhs, idx == 0, idx == nmm - 1, [lh.k(), hb.k()], [py.k()])
                            idx += 1
                    if eo == 0:
                        P.mm(out, k0m[:, cc, :], ub[:, cc, 0, :], False, True, [k0m.k(), ub.k()], [py.k()])

        def tailA(b):
            uf = ufl[b % 2]
            for cc in range(4):
                P.stt(zf[:, cc, :], uf[:, cc].rearrange("p e j -> p (e j)"), sp[:, SP_D + cc:SP_D + cc + 1],
                      py[:, cc * 128:(cc + 1) * 128], ALU.mult, ALU.add, [uf.k(), sp.k(), py.k()], [zf.k()])
            P.act(zf[:], zf[:], AF.Gelu_apprx_tanh, [zf.k()], [zf.k()])
            P.cp(zb[:], zf[:], [zf.k()], [zb.k()], eng='act')
            for oc in range(4):
                for k in range(4):
                    P.mm(pg_[:, oc * 128:(oc + 1) * 128], wglu[:, k, oc * 128:(oc + 1) * 128], zb[:, k, :],
                         k == 0, k == 3, [wglu.k(), zb.k()], [pg_.k()])
            for oc in range(4):
                P.act(sg[:, oc, :], pg_[:, oc * 128:(oc + 1) * 128], AF.Sigmoid, [pg_.k(), sp.k()], [sg.k()],
                      bias=sp[:, SP_BGLU + oc:SP_BGLU + oc + 1], scale=1.0)

        def tailB(b):
            c0, smp, tix = blk(b)
            yav = ya[:, :, c0:c0 + 128].rearrange("p c (j e) -> p c e j", e=2)
            P.tt(yav, zf[:].rearrange("p c (e j) -> p c e j", e=2), sg[:].rearrange("p c (e j) -> p c e j", e=2),
                 ALU.mult, [zf.k(), sg.k()], [ya.k(tix)])
            if b == NB - 2:
                P.dma('sp', o_ssm_p.ap[l], hp[:], [hp.k()], [o_ssm_p.k(l)], chan='hp')

        stage_U(0)
        stage_BU(0, 0)
        for b in range(NB):
            stage_ROT(b, 0)
            stage_BU(b, 1)
            stage_UNROT(b, 0)
            stage_CY(b, 0)
            if b > 0:
                tailB(b - 1)
            stage_ROT(b, 1)
            if b + 1 < NB:
                stage_U(b + 1)
                stage_BU(b + 1, 0)
            stage_UNROT(b, 1)
            stage_CY(b, 1)
            tailA(b)
        tailB(NB - 1)
        P.dma('sp', o_ssm_s.ap[l], hso[:], [hso.k()], [o_ssm_s.k(l)], chan='hso')
        P.release(m_ssm)
        if stop_here():
            break

        yb = P.alloc("yb", [128, 2, NT], BF16)
        m_g = P.mark()
        win_g = P.alloc("win_g", [128, KC, 512], BF16)
        wt_f = P.alloc("wt_f", [128, 2, 4, 128], F32)
        wtm = P.alloc("wtm", [128, 2, 4, 128], BF16)
        bct = P.alloc("bct", [128, NBC], F32)
        vnzl = [P.alloc(f"vnz{i}", [128, 2, 2, 128], BF16) for i in range(2)]
        ugl = [P.alloc(f"ug{i}", [128, 2, 128], F32) for i in range(3)]
        vgl = [P.alloc(f"vg{i}", [128, 256], F32) for i in range(2)]
        vnl = [P.alloc(f"vn{i}", [128, 256], F32) for i in range(2)]
        stt6 = P.alloc("stt6", [128, 6], F32)
        mv = P.alloc("mv", [128, 2], F32)
        sq_ = P.alloc("sq_", [128, 1], F32)
        ts_ = P.alloc("ts_", [128, 2, 128], F32)
        P.dma('pool', win_g[:], w_in.ap[l, :, :, O1:O3], [w_in.k()], [win_g.k()], chan='win_g')
        P.dma('sp', wt_f[:], wts.ap[l], [wts.k()], [wt_f.k()], chan='wt_f')
        P.dma('sp', bct[:], bcp.ap[l], [bcp.k()], [bct.k()], chan='bct')
        P.tt(wtm[:].rearrange("p a h t -> p (a h) t"), wt_f[:].rearrange("p a h t -> p (a h) t"),
             tril[:].unsqueeze(1).to_broadcast([128, 8, 128]), ALU.mult, [wt_f.k(), tril.k()], [wtm.k()])
        for v_ in vnzl:
            P.memset(v_[:], 0.0, [v_.k()])
        pu2, pv, pss = bank(0), bank(1), bank(2)

        def gA(b):
            c0 = b * 128
            tix = tile_of(c0)
            ug, vg = ugl[b % 3], vgl[b % 2]
            for cc in range(2):
                for k in range(KC):
                    P.mm(pu2[:, cc * 128:(cc + 1) * 128], win_g[:, k, cc * 128:(cc + 1) * 128], xb[:, k, c0:c0 + 128],
                         k == 0, k == KC - 1, [win_g.k(), xb.k(tix)], [pu2.k()])
            P.act(ug[:].rearrange("p a b -> p (a b)"), pu2[:, 0:256], AF.Gelu_apprx_tanh, [pu2.k()], [ug.k()])
            for k in range(KC):
                P.mm(pv[:, 0:256], xb[:, k, c0:c0 + 128], win_g[:, k, 256:512], k == 0, k == KC - 1,
                     [win_g.k(), xb.k(tix)], [pv.k()])
            P.act(vg[:], pv[:, 0:256], AF.Gelu_apprx_tanh, [pv.k()], [vg.k()])

        def gB(b):
            smp = (b == NB - 1)
            vg, vn, vnz = vgl[b % 2], vnl[b % 2], vnzl[b % 2]
            P.op('dve', lambda e: e.bn_stats(stt6[:], vg[:]), [vg.k()], [stt6.k()])
            P.op('dve', lambda e: e.bn_aggr(mv[:], stt6[:]), [stt6.k()], [mv.k()])
            P.ts(sq_[:], mv[:, 1:2], EPS, None, ALU.add, None, [mv.k()], [sq_.k()])
            P.act(sq_[:], sq_[:], AF.Sqrt, [sq_.k()], [sq_.k()])
            P.op('dve', lambda e: e.reciprocal(sq_[:], sq_[:]), [sq_.k()], [sq_.k()])
            P.ts(vn[:], vg[:], mv[:, 0:1], sq_[:, 0:1], ALU.subtract, ALU.mult, [vg.k(), mv.k(), sq_.k()], [vn.k()])
            P.tt(vn[:], vn[:], bct[:, BC_LNG:BC_LNG + 256], ALU.mult, [vn.k(), bct.k()], [vn.k()])
            P.tt(vn[:], vn[:], bct[:, BC_LNB:BC_LNB + 256], ALU.add, [vn.k(), bct.k()], [vn.k()])
            if smp:
                P.dma('sp', o_v_s.ap[l], vn[:], [vn.k()], [o_v_s.k(l)], chan='vn')
            vn4 = vn[:].rearrange("p (a h d) -> p a h d", a=2, h=2)
            P.cp(vnz[:, :, 0, 0:64], vn4[:, :, 0, :], [vn.k()], [vnz.k()])
            P.cp(vnz[:, :, 1, 64:128], vn4[:, :, 1, :], [vn.k()], [vnz.k()])

        def gC(b):
            c0 = b * 128
            smp = (b == NB - 1)
            tix = tile_of(c0)
            ug, vnz = ugl[b % 3], vnzl[b % 2]
            wsel = 1 if smp else 0
            for pr in range(2):
                for hh in range(2):
                    P.mm(pss[:, pr * 128:(pr + 1) * 128], vnz[:, pr, hh, :], wtm[:, wsel, 2 * pr + hh, :],
                         hh == 0, hh == 1, [vnz.k(), wtm.k()], [pss.k()])
            bso = BC_BSS if smp else BC_BSP
            P.tt(ts_[:].rearrange("p a b -> p (a b)"), pss[:, 0:256], bct[:, bso:bso + 256], ALU.add,
                 [pss.k(), bct.k()], [ts_.k()])
            P.tt(yb[:, :, c0:c0 + 128], ts_[:], ug[:], ALU.mult, [ts_.k(), ug.k()], [yb.k(tix)])

        gA(0)
        for b in range(NB):
            if b + 1 < NB:
                gA(b + 1)
            gB(b)
            if b > 0:
                gC(b - 1)
        gC(NB - 1)
        P.release(m_g)
        if stop_here():
            break

        yc = P.alloc("yc", [128, 4, NT], BF16)
        m_p = P.mark()
        win_p = P.alloc("win_p", [128, KC, 512], BF16)
        wpl = P.alloc("wpl", [128, 4, 128], BF16)
        xppl = [P.alloc(f"xpp{i}", [128, 16 + TP], F32) for i in range(2)]
        xpsl = [P.alloc(f"xps{i}", [128, 16, 24], F32) for i in range(2)]
        sa = P.alloc("sa", [128, 16 + TP], F32)
        sb_ = P.alloc("sb_", [128, 16 + TP], F32)
        ssa = P.alloc("ssa", [128, 16, 24], F32)
        ssb = P.alloc("ssb", [128, 16, 24], F32)
        dTl = [P.alloc(f"dT{i}", [128, NT], BF16) for i in range(2)]
        d16 = P.alloc("d16", [128, 16], F32)
        P.dma('pool', win_p[:], w_in.ap[l, :, :, O3:O4], [w_in.k()], [win_p.k()], chan='win_p')
        P.dma('pool', wpl[:], w_pool.ap[l], [w_pool.k()], [wpl.k()], chan='wpl')
        for x_ in xppl:
            P.memset(x_[:, 0:16], 0.0, [x_.k('pad')])

        def pool1(g):
            xpp, xps = xppl[g % 2], xpsl[g % 2]
            P.memset(xps[:, :, 0:1], 0.0, [xps.k('pad')])
            P.dma('sp', xps[:, :, 1:16], poolpast.ap[l, :, g], [poolpast.k()], [xps.k('pad')], chan=f'xps{g % 2}')
            for ti, (c0, n) in enumerate(TILES):
                pb = bank(ti % 4)
                for k in range(KC):
                    P.mm(pb[:, 0:n], win_p[:, k, g * 128:(g + 1) * 128], xb[:, k, c0:c0 + n], k == 0, k == KC - 1,
                         [win_p.k(), xb.k(ti)], [pb.k()])
                if ti < 4:
                    P.cp(xpp[:, 16 + c0:16 + c0 + n], pb[:, 0:n], [pb.k()], [xpp.k('d')], eng='act')
                else:
                    P.cp(xps[:, :, 16:24], pb[:, 0:128].rearrange("p (s t) -> p s t", t=8), [pb.k()], [xps.k('d')], eng='act')
            P.dma('sp', o_pool_p.ap[l, :, g], xpp[:, 16 + TP - 15:16 + TP], [xpp.k()], [o_pool_p.k((l, g))], chan=f'xpp{g % 2}')
            P.dma('sp', o_pool_s.ap[l, :, g], xps[:, :, 9:24], [xps.k()], [o_pool_s.k((l, g))], chan=f'xpso{g % 2}')

        def pool2(g):
            win = 2 ** (g + 1)
            xpp, xps, dT = xppl[g % 2], xpsl[g % 2], dTl[g % 2]
            curp, curs = xpp, xps
            bufs_p, bufs_s = [sa, sb_], [ssa, ssb]
            for kk in range(g + 1):
                sh = 2 ** kk
                lo = 2 ** (kk + 1) - 1
                np_, ns_ = bufs_p[kk % 2], bufs_s[kk % 2]
                P.tt(np_[:, lo:16 + TP], curp[:, lo:16 + TP], curp[:, lo - sh:16 + TP - sh], ALU.add, [curp.k()], [np_.k()])
                P.tt(ns_[:, :, lo:24], curs[:, :, lo:24], curs[:, :, lo - sh:24 - sh], ALU.add, [curs.k()], [ns_.k()])
                curp, curs = np_, ns_
            P.stt(dT[:, 0:TP], curp[:, 16:16 + TP], 1.0 / win, xpp[:, 16:16 + TP], ALU.mult, ALU.subtract,
                  [curp.k(), xpp.k()], [dT.k()])
            P.stt(dT[:, TP:NT].rearrange("p (s t) -> p s t", t=8), curs[:, :, 16:24], 1.0 / win, xps[:, :, 16:24],
                  ALU.mult, ALU.subtract, [curs.k(), xps.k()], [dT.k()])
            P.tt(d16[:], curp[:, 16:32], rc[:, g, :], ALU.mult, [curp.k(), rc.k()], [d16.k()])
            P.tt(dT[:, 0:16], d16[:], xpp[:, 16:32], ALU.subtract, [d16.k(), xpp.k()], [dT.k()])
            for ti, (c0, n) in enumerate(TILES):
                pb = bank(4 + ti % 4)
                P.mm(pb[:, 0:n], wpl[:, g, :], dT[:, c0:c0 + n], True, True, [wpl.k(), dT.k()], [pb.k()])
                P.act(yc[:, g, c0:c0 + n], pb[:, 0:n], AF.Identity, [pb.k(), sp.k()], [yc.k(ti)],
                      scale=sp[:, SP_PSC + g:SP_PSC + g + 1])

        pool1(0)
        for g in range(4):
            if g + 1 < 4:
                pool1(g + 1)
            pool2(g)
        P.release(m_p)
        if stop_here():
            break

        mg = P.alloc("mg", [128, KC, NT], BF16)
        m_m = P.mark()
        wgt = [P.alloc(f"wgt{i}", [128, KC, 3, 128], BF16) for i in range(2)]
        wbt = [P.alloc(f"wbt{i}", [128, 10, 128], BF16) for i in range(2)]
        sgt = [P.alloc(f"sgt{i}", [128, 512], F32) for i in range(2)]
        macc = P.alloc("macc", [128, 512], F32)
        mtmp = P.alloc("mtmp", [128, 512], F32)
        ybr = [(ya, 0, 4), (yb, 4, 2), (yc, 6, 4)]
        step = 0
        for dc in range(KC):
            wg, wb = wgt[dc % 2], wbt[dc % 2]
            P.dma('pool', wg[:], w_in.ap[l, :, :, O4:INC].rearrange("p k (i n) -> p k i n", i=3)[:, :, :, dc * 128:(dc + 1) * 128],
                  [w_in.k()], [wg.k()], chan=f'wgt{dc % 2}')
            P.dma('pool', wb[:], w_br.ap[l, :, :, dc * 128:(dc + 1) * 128], [w_br.k()], [wb.k()], chan=f'wbt{dc % 2}')
            for ti, (c0, n) in enumerate(TILES):
                for i in range(3):
                    pgt, pbt = bank((2 * step) % 8), bank((2 * step + 1) % 8)
                    sgi = sgt[step % 2]
                    step += 1
                    for k in range(KC):
                        P.mm(pgt[:, 0:n], wg[:, k, i, :], xb[:, k, c0:c0 + n], k == 0, k == KC - 1, [wg.k(), xb.k(ti)], [pgt.k()])
                    ysrc, koff, nk = ybr[i]
                    for k in range(nk):
                        P.mm(pbt[:, 0:n], wb[:, koff + k, :], ysrc[:, k, c0:c0 + n], k == 0, k == nk - 1,
                             [wb.k(), ysrc.k(ti)], [pbt.k()])
                    P.act(sgi[:, 0:n], pgt[:, 0:n], AF.Sigmoid, [pgt.k(), sp.k()], [sgi.k()],
                          bias=sp[:, SP_BG + i * 8 + dc:SP_BG + i * 8 + dc + 1], scale=1.0)
                    if i == 0:
                        P.tt(macc[:, 0:n], pbt[:, 0:n], sgi[:, 0:n], ALU.mult, [pbt.k(), sgi.k()], [macc.k()])
                    elif i == 1:
                        P.tt(mtmp[:, 0:n], pbt[:, 0:n], sgi[:, 0:n], ALU.mult, [pbt.k(), sgi.k()], [mtmp.k()])
                        P.tt(macc[:, 0:n], macc[:, 0:n], mtmp[:, 0:n], ALU.add, [macc.k(), mtmp.k()], [macc.k()])
                    else:
                        P.tt(mtmp[:, 0:n], pbt[:, 0:n], sgi[:, 0:n], ALU.mult, [pbt.k(), sgi.k()], [mtmp.k()])
                        P.tt(mg[:, dc, c0:c0 + n], macc[:, 0:n], mtmp[:, 0:n], ALU.add, [macc.k(), mtmp.k()], [mg.k(ti)])
        P.release(m_m)
        if stop_here():
            break

        ln_m = dict(xt=[P.alloc(f"xt{i}", [128, KC, 512], F32) for i in range(2)],
                    xbt=P.alloc("xbt", [128, KC, 512], BF16), sqt=P.alloc("sqt", [128, KC, 512], BF16),
                    mean=P.alloc("mean", [128, 512], F32), var=P.alloc("var", [128, 512], F32),
                    rstd=P.alloc("rstd", [128, 512], F32))
        wo = P.alloc("wo", [128, KC, D], BF16)
        P.dma('pool', wo[:], w_o.ap[l], [w_o.k()], [wo.k()], chan='wo')
        def mm_wo(ti, dc, pb):
            c0, n = TILES[ti]
            for k in range(KC):
                P.mm(pb[:, 0:n], wo[:, k, dc * 128:(dc + 1) * 128], mg[:, k, c0:c0 + n], k == 0, k == KC - 1,
                     [wo.k(), mg.k(ti)], [pb.k()])
        resid_ln_seq(l, [0, 1, 2, 3, 4], mm_wo, src0, mid, SP_LN1G, SP_LN1B, ln_m, 'p4', 'act')()
        fin_hold = [None]
        P.release(base_mark)
        if stop_here():
            break

        ln_m = dict(xt=[P.alloc(f"xt{i}", [128, KC, 512], F32) for i in range(2)],
                    xbt=P.alloc("xbt", [128, KC, 512], BF16), sqt=P.alloc("sqt", [128, KC, 512], BF16),
                    mean=P.alloc("mean", [128, 512], F32), var=P.alloc("var", [128, 512], F32),
                    rstd=P.alloc("rstd", [128, 512], F32))
        HALVES = [[0, 1], [2, 3, 4]]
        wup = [P.alloc(f"wup{i}", [128, KC, 2, 128], BF16) for i in range(2)]
        wdn = [P.alloc(f"wdn{i}", [128, NPAIR, 128], BF16) for i in range(3)]
        gbuf = P.alloc("gbuf", [128, NPAIR, 1152], BF16)
        hfl = [[P.alloc(f"hf{j}{i}", [128, 2 + 1024], F32) for i in range(2)] for j in range(2)]
        hsl = [[P.alloc(f"hs{j}{i}", [128, 16, 10], F32) for i in range(2)] for j in range(2)]
        cy = [P.alloc(f"cy{j}", [128, 1152], F32) for j in range(2)]
        cyb = [P.alloc_at(f"cyb{j}", [128, 1152], F32, ln_m['xbt'].lo + j * 4608) for j in range(2)]
        assert cyb[1].hi <= ln_m['rstd'].hi
        cysets = [cy, cyb]
        hcar = P.alloc("hcar", [128, 44, 2], F32)
        cpast = P.alloc("cpast", [128, 44, 16, 2], F32)
        csout = P.alloc("csout", [128, 44, 16, 2], F32)
        csp = P.alloc("csp", [128, 44, 2], F32)
        P.dma('sp', cpast[:], convpast.ap[l], [convpast.k()], [cpast.k()], chan='cpast')
        P.memset(hcar[:], 0.0, [hcar.k()])
        ui = 0
        di = 0
        for hi, tl in enumerate(HALVES):
            ptl = [t for t in tl if t < 4]
            NP_ = 512 * len(ptl)
            has_s = 4 in tl
            NN = NP_ + (128 if has_s else 0)
            def mm_evac_pieces(pi):
                nonlocal ui
                wu = wup[ui % 2]
                cn = f'wup{ui % 2}'
                ui += 1
                P.dma('pool', wu[:], w_up.ap[l].rearrange("p k (h n) -> p k h n", h=2)[:, :, :, pi * 128:(pi + 1) * 128],
                      [w_up.k()], [wu.k()], chan=cn)
                pieces = []
                for j in range(2):
                    hfb = hfl[j][pi % 2]
                    hs_ = hsl[j][pi % 2]
                    for q, ti in enumerate(tl):
                        def piece(j=j, q=q, ti=ti, hfb=hfb, hs_=hs_, wu=wu):
                            c0, n = TILES[ti]
                            pb = bank(j * 3 + q)
                            for k in range(KC):
                                P.mm(pb[:, 0:n], wu[:, k, j, :], xb[:, k, c0:c0 + n], k == 0, k == KC - 1, [wu.k(), xb.k(ti)], [pb.k()])
                            if ti < 4:
                                P.cp(hfb[:, 2 + q * 512:2 + q * 512 + n], pb[:, 0:n], [pb.k()], [hfb.k('d')], eng='act')
                            else:
                                P.cp(hs_[:, :, 2:10], pb[:, 0:128].rearrange("p (s t) -> p s t", t=8), [pb.k()], [hs_.k('d')], eng='act')
                        pieces.append(piece)
                return pieces

            def tail_pieces(pi):
                cys = cysets[pi % 2]
                prm = []
                for j, chn in ((0, pi), (1, NPAIR + pi)):
                    w0 = sp[:, SP_CW + chn * 3 + 0:SP_CW + chn * 3 + 1]
                    w1 = sp[:, SP_CW + chn * 3 + 1:SP_CW + chn * 3 + 2]
                    w2 = sp[:, SP_CW + chn * 3 + 2:SP_CW + chn * 3 + 3]
                    cb = sp[:, SP_CB + chn:SP_CB + chn + 1]
                    prm.append((j, chn, w0, w1, w2, cb, hfl[j][pi % 2], hsl[j][pi % 2], cys[j]))

                def t_carry():
                    for (j, chn, w0, w1, w2, cb, hfb, hs_, cyj) in prm:
                        P.cp(hfb[:, 0:2], hcar[:, chn, :], [hcar.k(chn)], [hfb.k('c')])
                        if has_s:
                            P.cp(hs_[:, :, 0:2], cpast[:, chn, :, :], [cpast.k()], [hs_.k('c')])

                def t_ident(jj, smp_part):
                    (j, chn, w0, w1, w2, cb, hfb, hs_, cyj) = prm[jj]
                    if not smp_part:
                        P.act(cyj[:, 0:NP_], hfb[:, 0:NP_], AF.Identity, [hfb.k(), sp.k()], [cyj.k('p')], bias=cb, scale=w0)
                    elif has_s:
                        oS = cyj[:, NP_:NP_ + 128].rearrange("p (s t) -> p s t", t=8)
                        P.act(oS, hs_[:, :, 0:8], AF.Identity, [hs_.k(), sp.k()], [cyj.k('s')], bias=cb, scale=w0)

                def t_stt():
                    for (j, chn, w0, w1, w2, cb, hfb, hs_, cyj) in prm:
                        oP = cyj[:, 0:NP_]
                        P.stt(oP, hfb[:, 1:NP_ + 1], w1, oP, ALU.mult, ALU.add, [hfb.k(), sp.k(), cyj.k('p')], [cyj.k('p')])
                        P.stt(oP, hfb[:, 2:NP_ + 2], w2, oP, ALU.mult, ALU.add, [hfb.k(), sp.k(), cyj.k('p')], [cyj.k('p')])
                        if hi == 0:
                            P.cp(hcar[:, chn, :], hfb[:, NP_:NP_ + 2], [hfb.k()], [hcar.k(chn)])
                        else:
                            P.cp(csp[:, chn, :], hfb[:, NP_:NP_ + 2], [hfb.k()], [csp.k(chn)])
                    if has_s:
                        for (j, chn, w0, w1, w2, cb, hfb, hs_, cyj) in prm:
                            oS = cyj[:, NP_:NP_ + 128].rearrange("p (s t) -> p s t", t=8)
                            P.stt(oS, hs_[:, :, 1:9], w1, oS, ALU.mult, ALU.add, [hs_.k(), sp.k(), cyj.k('s')], [cyj.k('s')])
                            P.stt(oS, hs_[:, :, 2:10], w2, oS, ALU.mult, ALU.add, [hs_.k(), sp.k(), cyj.k('s')], [cyj.k('s')])
                            P.cp(csout[:, chn, :, :], hs_[:, :, 8:10], [hs_.k()], [csout.k(chn)])

                def t_gelu():
                    P.act(cys[1][:, 0:NN], cys[1][:, 0:NN], AF.Gelu_apprx_tanh, [cys[1].k()], [cys[1].k()])

                def t_mult():
                    P.tt(gbuf[:, pi, 0:NN], cys[1][:, 0:NN], cys[0][:, 0:NN], ALU.mult, [cys[1].k(), cys[0].k()], [gbuf.k(pi)])
                return [t_carry, lambda: t_ident(0, False), lambda: t_ident(1, False), lambda: t_ident(0, True),
                        lambda: t_ident(1, True), t_stt, t_gelu, t_mult]

            for pc in mm_evac_pieces(0):
                pc()
            prev_tp = None
            for pi in range(NPAIR):
                tp = tail_pieces(pi)
                mp = mm_evac_pieces(pi + 1) if pi + 1 < NPAIR else []
                order = [tp[0]]
                ai = 0
                for t_ in tp[1:5]:
                    if ai < len(mp):
                        order.append(mp[ai])
                        ai += 1
                    order.append(t_)
                if prev_tp is not None:
                    order.append(prev_tp[6])
                while ai < len(mp):
                    order.append(mp[ai])
                    ai += 1
                order.append(tp[5])
                if prev_tp is not None:
                    order.append(prev_tp[7])
                for f_ in order:
                    f_()
                prev_tp = tp
                if pi == 1 and fin_hold[0] is not None:
                    fin_hold[0]()
                    fin_hold[0] = None
            prev_tp[6]()
            prev_tp[7]()
            goffs = {ti: (q * 512 if ti < 4 else NP_) for q, ti in enumerate(tl)}

            def mm_dn(ti, dc, pb, goffs=goffs):
                nonlocal di
                c0, n = TILES[ti]
                goff = goffs[ti]
                wd = wdn[di % 3]
                P.dma('pool', wd[:], w_dn.ap[l, :, :, dc * 128:(dc + 1) * 128], [w_dn.k()], [wd.k()], chan=f'wdn{di % 3}')
                di += 1
                for k in range(NPAIR):
                    P.mm(pb[:, 0:n], wd[:, k, :], gbuf[:, k, goff:goff + n], k == 0, k == NPAIR - 1, [wd.k(), gbuf.k(k)], [pb.k()])
            fin_ = resid_ln_seq(l, tl, mm_dn, mid, dst2, SP_LN2G, SP_LN2B, ln_m, 'p4', ('act' if l < L - 1 else 'none'))
            if hi == 0:
                fin_hold[0] = fin_
            else:
                fin_()
        P.dma('sp', o_conv_p.ap[l], csp[:], [csp.k()], [o_conv_p.k(l)], chan='csp')
        P.dma('sp', o_conv_s.ap[l], csout[:], [csout.k()], [o_conv_s.k(l)], chan='csout')
        P.release(base_mark)

    P.op('sp', lambda e: e.nop(), [], [o.k() for o in outs])


_CACHE = {}


def _get_program():
    if 'nc' not in _CACHE:
        _CACHE['nc'], _CACHE['stats'] = build_program()
    return _CACHE['nc']


def _kc(w):
    K, N = w.shape
    return np.ascontiguousarray(w.reshape(K // 128, 128, N).transpose(1, 0, 2))


def _chunkvec(v):
    return np.ascontiguousarray(v.reshape(-1, 128).T)


def _shared_inputs(inp):
    f = np.float32
    sh = {}
    sh['w_in'] = np.stack([_kc(inp['w_in'][l]) for l in range(L)])
    sh['w_br'] = np.stack([_kc(np.concatenate([inp['w_br_ssm'][l], inp['w_br_gmlp'][l], inp['w_br_pool'][l]], 0))
                           for l in range(L)])
    sh['w_o'] = np.stack([_kc(inp['w_o'][l]) for l in range(L)])
    sh['w_glu'] = np.stack([_kc(inp['ssm_w_glu'][l]) for l in range(L)])
    sh['w_up'] = np.stack([_kc(inp['ffn_w_up'][l]) for l in range(L)])
    sh['w_dn'] = np.stack([_kc(inp['ffn_w_down'][l]) for l in range(L)])
    sh['w_pool'] = np.ascontiguousarray(np.transpose(inp['pool_w'], (0, 2, 1, 3)))
    bpad = np.zeros((L, 128, 2, 16, 128), f)
    cpad = np.zeros((L, 128, 2, 16, 128), f)
    bpad2 = np.zeros((L, 128, 2, 16, 128), f)
    for ri, (bn, cn) in enumerate((('ssm_b_re', 'ssm_c_re'), ('ssm_b_im', 'ssm_c_im'))):
        B = inp[bn]
        C = inp[cn]
        for g in range(32):
            gp, g2 = g // 2, g % 2
            r0 = 32 * (gp % 4) + 16 * g2
            bpad[:, r0:r0 + 16, ri, gp, 64 * g2:64 * g2 + 64] = np.transpose(B[:, g], (0, 2, 1))
            cpad[:, 64 * g2:64 * g2 + 64, ri, gp, r0:r0 + 16] = np.transpose(C[:, g], (0, 2, 1))
            bpad2[:, 64 * g2:64 * g2 + 64, ri, gp, r0:r0 + 16] = B[:, g]
    sh['bpad'] = bpad
    sh['cpad'] = cpad
    sh['bpad2'] = bpad2
    wts = np.zeros((L, 128, 2, 4, 128), f)
    wts[:, :, 0] = np.transpose(inp['gmlp_w_s'], (0, 3, 1, 2))
    for s in range(16):
        wts[:, 8 * s:8 * s + 8, 1, :, 8 * s:8 * s + 8] = np.transpose(inp['gmlp_w_s'][:, :, :8, :8], (0, 3, 1, 2))
    sh['wts'] = wts
    sp = np.zeros((L, 128, NSP), f)
    bc = np.zeros((L, 128, NBC), f)
    for l in range(L):
        sp[l, :, SP_D:SP_D + 4] = _chunkvec(inp['ssm_d'][l].reshape(-1))
        sp[l, :, SP_BGLU:SP_BGLU + 4] = _chunkvec(inp['ssm_b_glu'][l])
        sp[l, :, SP_PSC:SP_PSC + 4] = _chunkvec(inp['pool_scale'][l])
        sp[l, :, SP_BG:SP_BG + 24] = _chunkvec(inp['b_gate'][l].reshape(-1))
        sp[l, :, SP_LN1G:SP_LN1G + 8] = _chunkvec(inp['ln1_g'][l])
        sp[l, :, SP_LN1B:SP_LN1B + 8] = _chunkvec(inp['ln1_b'][l])
        sp[l, :, SP_LN2G:SP_LN2G + 8] = _chunkvec(inp['ln2_g'][l])
        sp[l, :, SP_LN2B:SP_LN2B + 8] = _chunkvec(inp['ln2_b'][l])
        cw = inp['ffn_conv_w'][l]
        sp[l, :, SP_CW:SP_CW + 132] = np.transpose(cw.reshape(3, 44, 128), (2, 1, 0)).reshape(128, 132)
        sp[l, :, SP_CB:SP_CB + 44] = _chunkvec(inp['ffn_conv_b'][l])
        sp[l, :, SP_ARE:SP_ARE + 16] = inp['ssm_a_re'][l].reshape(16, 128).T
        sp[l, :, SP_AIM:SP_AIM + 16] = inp['ssm_a_im'][l].reshape(16, 128).T
        sp[l, :, SP_LS:SP_LS + 16] = np.repeat(inp['ssm_log_step'][l].reshape(16, 2), 64, axis=1).T
        bc[l, :, BC_LNG:BC_LNG + 256] = inp['gmlp_ln_g'][l][None, :]
        bc[l, :, BC_LNB:BC_LNB + 256] = inp['gmlp_ln_b'][l][None, :]
        bs = inp['gmlp_b_s'][l]
        for pr in range(2):
            for hh in range(2):
                bc[l, 64 * hh:64 * hh + 64, BC_BSP + pr * 128:BC_BSP + (pr + 1) * 128] = bs[2 * pr + hh][None, :]
                bc[l, 64 * hh:64 * hh + 64, BC_BSS + pr * 128:BC_BSS + (pr + 1) * 128] = np.tile(bs[2 * pr + hh, :8], 16)[None, :]
    sh['smallp'] = sp
    sh['bcp'] = bc
    rc = np.zeros((128, 4, 16), f)
    for g in range(4):
        rc[:, g, :] = (1.0 / np.minimum(np.arange(16) + 1, 2 ** (g + 1))).astype(f)[None, :]
    sh['rcc'] = rc
    return sh


def _prep(inp):
    sh = _shared_inputs(inp)
    in_maps = []
    for c in range(8):
        m = dict(sh)
        xp = inp['x_prompt'][c]
        xs = inp['x_sample'][16 * c:16 * c + 16].reshape(128, D)
        xa = np.concatenate([xp, xs], 0)
        m['xT0'] = np.ascontiguousarray(xa.reshape(NT, KC, 128).transpose(2, 1, 0))
        sl = slice(16 * c, 16 * c + 16)
        hr = inp['state_ssm_re'][:, sl].reshape(L, 16, 16, 128)
        hi = inp['state_ssm_im'][:, sl].reshape(L, 16, 16, 128)
        m['h0'] = np.ascontiguousarray(np.stack([hr, hi], 1).transpose(0, 4, 1, 3, 2))
        pp = inp['state_pool'][:, sl].reshape(L, 16, 15, 4, 128)
        m['poolpast'] = np.ascontiguousarray(pp.transpose(0, 4, 3, 1, 2))
        cp = inp['state_ffn_conv'][:, sl].reshape(L, 16, 2, 44, 128)
        m['convpast'] = np.ascontiguousarray(cp.transpose(0, 4, 3, 1, 2))
        in_maps.append(m)
    return in_maps


def kernel(**inp):
    inp = {k: np.asarray(v) for k, v in inp.items()}
    nc = _get_program()
    in_maps = _prep(inp)
    resu = run_bass_kernel_spmd(nc, in_maps, core_ids=list(range(8)))
    return _post(resu.results)


def _post(R):
    f = np.float32
    y_p = np.zeros((8, TP, D), f)
    y_s = np.zeros((128, 8, D), f)
    ssm_re_p = np.zeros((L, 8, 32, 64), f)
    ssm_im_p = np.zeros((L, 8, 32, 64), f)
    pool_p = np.zeros((L, 8, 15, 512), f)
    conv_p = np.zeros((L, 8, 2, F2), f)
    ssm_re_s = np.zeros((L, 128, 32, 64), f)
    ssm_im_s = np.zeros((L, 128, 32, 64), f)
    v_s = np.zeros((L, 128, 8, 256), f)
    pool_s = np.zeros((L, 128, 15, 512), f)
    conv_s = np.zeros((L, 128, 2, F2), f)
    for c in range(8):
        r = R[c]
        ya = r['yT'].transpose(2, 1, 0).reshape(NT, D)
        y_p[c] = ya[:TP]
        y_s[16 * c:16 * c + 16] = ya[TP:].reshape(16, 8, D)
        sl = slice(16 * c, 16 * c + 16)
        sp_ = r['o_ssm_p']
        ssm_re_p[:, c] = sp_[:, :, 0].transpose(0, 2, 1).reshape(L, 32, 64)
        ssm_im_p[:, c] = sp_[:, :, 1].transpose(0, 2, 1).reshape(L, 32, 64)
        ss = r['o_ssm_s']
        ssm_re_s[:, sl] = ss[:, :, 0].transpose(0, 3, 2, 1).reshape(L, 16, 32, 64)
        ssm_im_s[:, sl] = ss[:, :, 1].transpose(0, 3, 2, 1).reshape(L, 16, 32, 64)
        pool_p[:, c] = r['o_pool_p'].transpose(0, 3, 2, 1).reshape(L, 15, 512)
        pool_s[:, sl] = r['o_pool_s'].transpose(0, 3, 4, 2, 1).reshape(L, 16, 15, 512)
        conv_p[:, c] = r['o_conv_p'].transpose(0, 3, 2, 1).reshape(L, 2, F2)
        conv_s[:, sl] = r['o_conv_s'].transpose(0, 3, 4, 2, 1).reshape(L, 16, 2, F2)
        v_s[:, sl] = r['o_v_s'].reshape(L, 16, 8, 256)
    return (y_p, y_s, ssm_re_p, ssm_im_p, pool_p, conv_p, ssm_re_s, ssm_im_s, v_s, pool_s, conv_s)
```

```python
import math
import os
import sys
from contextlib import ExitStack
import numpy as np
import concourse.bass as bass
import concourse.mybir as mybir
from concourse.bass_utils import run_bass_kernel_spmd

F32 = mybir.dt.float32
BF16 = mybir.dt.bfloat16
I32 = mybir.dt.int32
AF = mybir.ActivationFunctionType
ALU = mybir.AluOpType

L = 2
D = 1024
KC = 8
TP = 2048
NSM = 128
NT = TP + NSM
INC = 4608
O1, O2, O3, O4 = 512, 768, 1024, 1536
DFF = 2816
F2 = 5632
NPAIR = 22
ALPHA = float((2 * L) ** 0.25)
EPS = 1e-5
TILES = [(0, 512), (512, 512), (1024, 512), (1536, 512), (2048, 128)]
NB = 17
SP_D, SP_BGLU, SP_PSC, SP_BG, SP_LN1G, SP_LN1B, SP_LN2G, SP_LN2B = 0, 4, 8, 12, 36, 44, 52, 60
SP_CW, SP_CB, SP_ARE, SP_AIM, SP_LS = 68, 200, 244, 260, 276
NSP = 292
BC_LNG, BC_LNB, BC_BSP, BC_BSS = 0, 256, 512, 768
NBC = 1024


class Buf:
    def __init__(self, bid, space, lo, hi, ap, name):
        self.id, self.space, self.lo, self.hi, self.ap, self.name = bid, space, lo, hi, ap, name

    def k(self, sub=None):
        return (self.id, sub)

    def __getitem__(self, idx):
        return self.ap[idx]


class Prog:
    def __init__(self, nc, st, arena_bytes):
        self.nc, self.st = nc, st
        self.ops = []
        self.tags = []
        self.engs = {'pe': nc.tensor, 'act': nc.scalar, 'dve': nc.vector, 'pool': nc.gpsimd, 'sp': nc.sync}
        self.bufs = []
        self.alias = {}
        self.arena = st.enter_context(nc.sbuf_tensor("arena", [128, arena_bytes // 4], F32))
        self.arena_bytes = arena_bytes
        self.top = 0
        self.psum = st.enter_context(nc.psum_tensor("psa", [128, 8 * 512], F32))

    def _mk(self, space, lo, hi, ap, name):
        b = Buf(len(self.bufs), space, lo, hi, ap, name)
        al = set()
        for o in self.bufs:
            if o.space == space and o.lo < hi and lo < o.hi:
                al.add(o.id)
                self.alias[o.id].add(b.id)
        self.alias[b.id] = al
        self.bufs.append(b)
        return b

    def alloc(self, name, shape, dt=F32):
        esz = 2 if dt == BF16 else 4
        n = 1
        for s in shape[1:]:
            n *= s
        nbytes = (n * esz + 31) // 32 * 32
        lo = self.top
        hi = lo + nbytes
        assert hi <= self.arena_bytes, f"arena overflow at {name}: {hi}"
        self.top = hi
        ap = self.arena[:, lo // 4:(lo + n * esz + 3) // 4]
        if dt != F32:
            ap = ap.bitcast(dt)
            ap = ap[:, 0:n]
        if len(shape) > 2:
            names = "abcd"[:len(shape) - 1]
            pat = "p (" + " ".join(names) + ") -> p " + " ".join(names)
            ap = ap.rearrange(pat, **{nm: s for nm, s in zip(names, shape[1:])})
        if shape[0] < 128:
            ap = ap[0:shape[0]]
        return self._mk('sb', lo, hi, ap, name)

    def alloc_at(self, name, shape, dt, lo):
        top = self.top
        self.top = lo
        b = self.alloc(name, shape, dt)
        self.top = top
        return b

    def mark(self):
        return self.top

    def release(self, m):
        self.top = m

    def psum_buf(self, name, col0, ncols, shape=None):
        ap = self.psum[:, col0:col0 + ncols]
        if shape is not None and len(shape) > 2:
            names = "abcd"[:len(shape) - 1]
            pat = "p (" + " ".join(names) + ") -> p " + " ".join(names)
            ap = ap.rearrange(pat, **{nm: s for nm, s in zip(names, shape[1:])})
        return self._mk('ps', col0 * 4, (col0 + ncols) * 4, ap, name)

    def dram(self, name, shape, kind):
        t = self.nc.dram_tensor(name, list(shape), F32, kind=kind)
        return self._mk('dram_' + name, 0, 1, t.ap(), name)

    def op(self, eng, fn, r=(), w=(), dma=False, chan=None):
        f = sys._getframe(1)
        while f.f_code.co_name in ('dma', 'mm', 'act', 'tt', 'ts', 'stt', 'cp', 'memset', 'op'):
            f = f.f_back
        self.tags.append(f.f_lineno)
        self.ops.append((eng, fn, list(r), list(w), dma, chan))

    def dma(self, q, out, in_, r=(), w=(), chan=None):
        self.op(q, lambda e: e.dma_start(out=out, in_=in_), r, w, dma=True, chan=chan)

    def mm(self, out, lhsT, rhs, start, stop, r, w):
        self.op('pe', lambda e: e.matmul(out, lhsT, rhs, start=start, stop=stop), r, w)

    def act(self, out, in_, func, r, w, bias=None, scale=None):
        kw = {}
        if bias is not None:
            kw['bias'] = bias
        if scale is not None:
            kw['scale'] = scale
        self.op('act', lambda e: e.activation(out, in_, func, **kw), r, w)

    def tt(self, out, a, b, op, r, w, eng='dve'):
        self.op(eng, lambda e: e.tensor_tensor(out, a, b, op), r, w)

    def ts(self, out, a, s1, s2, op0, op1, r, w, eng='dve'):
        if op1 is None:
            self.op(eng, lambda e: e.tensor_scalar(out, a, s1, None, op0), r, w)
        else:
            self.op(eng, lambda e: e.tensor_scalar(out, a, s1, s2, op0, op1), r, w)

    def stt(self, out, a, s, b, op0, op1, r, w):
        self.op('dve', lambda e: e.scalar_tensor_tensor(out, a, s, b, op0, op1), r, w)

    def cp(self, out, in_, r, w, eng='dve'):
        if eng == 'act':
            self.op('act', lambda e: e.copy(out, in_), r, w)
        else:
            self.op(eng, lambda e: e.tensor_copy(out, in_), r, w)

    def memset(self, ap, val, w, eng='dve'):
        self.op(eng, lambda e: e.memset(ap, val), (), w)

    def _conf(self, state, key):
        bid, sub = key
        for (b2, s2), stt_ in state.get(bid, {}).items():
            if sub is None or s2 is None or s2 == sub:
                yield stt_
        for a in self.alias[bid]:
            for stt_ in state.get(a, {}).values():
                yield stt_

    def emit(self):
        ops = self.ops
        n = len(ops)
        state = {}
        deps = [None] * n
        for i, (eng, fn, rd, wr, isdma, chan) in enumerate(ops):
            d = set()
            for k in rd:
                for s_ in self._conf(state, k):
                    if s_[0] is not None:
                        d.add(s_[0])
            for k in wr:
                for s_ in self._conf(state, k):
                    if s_[0] is not None:
                        d.add(s_[0])
                    d.update(s_[1].values())
                    d.update(s_[2])
            d.discard(i)
            if eng == 'pe':
                d = {x for x in d if ops[x][0] != 'pe'}
            deps[i] = d
            for k in rd:
                s_ = state.setdefault(k[0], {}).setdefault(k, [None, {}, []])
                if isdma:
                    s_[2].append(i)
                else:
                    s_[1][eng] = i
            for k in wr:
                bid, sub = k
                for (b2, s2), s_ in state.get(bid, {}).items():
                    if sub is None or s2 is None or s2 == sub:
                        s_[0] = i
                        s_[1] = {}
                        s_[2] = []
                state.setdefault(bid, {})[k] = [i, {}, []]
        sig = set()
        for d in deps:
            sig.update(d)
        sems = {}

        def get_sem(name):
            if name not in sems:
                sems[name] = self.st.enter_context(self.nc.semaphore(name))
            return sems[name]
        eng_cnt, dma_cnt, signal, waited = {}, {}, {}, {}
        n_wait = 0
        for i, (eng, fn, rd, wr, isdma, chan) in enumerate(ops):
            eobj = self.engs[eng]
            wd = waited.setdefault(eng, {})
            need = {}
            for d in deps[i]:
                sn, v, isd = signal[d]
                if isd:
                    v = dma_cnt[sn]
                if wd.get(sn, 0) < v:
                    need[sn] = max(need.get(sn, 0), v)
            for sn, v in need.items():
                eobj.wait_ge(get_sem(sn), v)
                wd[sn] = v
                n_wait += 1
            inst = fn(eobj)
            if isdma:
                sn = 'd_' + str(chan)
                dma_cnt[sn] = dma_cnt.get(sn, 0) + 16
                inst.then_inc(get_sem(sn), 16)
                signal[i] = (sn, dma_cnt[sn], True)
            elif i in sig:
                sn = 'e_' + eng
                eng_cnt[sn] = eng_cnt.get(sn, 0) + 1
                inst.then_inc(get_sem(sn), 1)
                signal[i] = (sn, eng_cnt[sn], False)
        self.deps = deps
        self.signal = signal
        self.stats = dict(n_ops=n, n_wait=n_wait, n_sems=len(sems), eng_cnt=eng_cnt,
                          n_dma=sum(v for v in dma_cnt.values()) // 16)
        return self.stats


def build_program():
    nc = bass.Bass("TRN2", target_bir_lowering=False)
    st = ExitStack()
    with st:
        P = Prog(nc, st, 212480)
        _build(nc, P)
        stats = P.emit()
        _CACHE['P'] = P
    return nc, stats


def _build(nc, P):
    def din(name, shape):
        return P.dram(name, shape, "ExternalInput")

    def dout(name, shape):
        return P.dram(name, shape, "ExternalOutput")

    xT0 = din("xT0", [128, KC, NT])
    h0 = din("h0", [L, 128, 2, 16, 16])
    poolpast = din("poolpast", [L, 128, 4, 16, 15])
    convpast = din("convpast", [L, 128, 44, 16, 2])
    w_in = din("w_in", [L, 128, KC, INC])
    w_br = din("w_br", [L, 128, 10, D])
    w_o = din("w_o", [L, 128, KC, D])
    w_glu = din("w_glu", [L, 128, 4, 512])
    w_up = din("w_up", [L, 128, KC, F2])
    w_dn = din("w_dn", [L, 128, NPAIR, D])
    w_pool = din("w_pool", [L, 128, 4, 128])
    bpad = din("bpad", [L, 128, 2, 16, 128])
    cpad = din("cpad", [L, 128, 2, 16, 128])
    bpad2 = din("bpad2", [L, 128, 2, 16, 128])
    wts = din("wts", [L, 128, 2, 4, 128])
    smallp = din("smallp", [L, 128, NSP])
    bcp = din("bcp", [L, 128, NBC])
    rcc = din("rcc", [128, 4, 16])

    yT = dout("yT", [128, KC, NT])
    o_ssm_p = dout("o_ssm_p", [L, 128, 2, 16])
    o_ssm_s = dout("o_ssm_s", [L, 128, 2, 16, 16])
    o_pool_p = dout("o_pool_p", [L, 128, 4, 15])
    o_pool_s = dout("o_pool_s", [L, 128, 4, 16, 15])
    o_conv_p = dout("o_conv_p", [L, 128, 44, 2])
    o_conv_s = dout("o_conv_s", [L, 128, 44, 16, 2])
    o_v_s = dout("o_v_s", [L, 128, 256])
    res = [P.dram(f"res{i}", [128, KC, NT], "Internal") for i in range(3)]
    outs = [yT, o_ssm_p, o_ssm_s, o_pool_p, o_pool_s, o_conv_p, o_conv_s, o_v_s]

    PS = [P.psum_buf(f"ps{i}", i * 512, 512) for i in range(8)]

    def bank(i):
        return PS[i]

    xb = P.alloc("xb", [128, KC, NT], BF16)
    ones_bf = P.alloc("ones_bf", [128, 128], BF16)
    ones_f = P.alloc("ones_f", [128, 128], F32)
    ident = P.alloc("ident", [128, 128], F32)
    tril = P.alloc("tril", [128, 128], F32)
    iot = P.alloc("iot", [128, 128], F32)
    rc = P.alloc("rc", [128, 4, 16], F32)
    spm = [P.alloc(f"spm{l}", [128, NSP], F32) for l in range(L)]
    base_mark = P.mark()

    def tile_of(c0):
        return min(c0 // 512, 4)

    iot_i = P.alloc("iot_i", [128, 128], I32)
    P.memset(ones_bf[:], 1.0, [ones_bf.k()])
    P.memset(ones_f[:], 1.0, [ones_f.k()])
    P.op('pool', lambda e: e.affine_select(tril[:], ones_f[:], [[1, 128]], ALU.is_ge, 0.0, base=0,
                                           channel_multiplier=-1), [ones_f.k()], [tril.k()])
    P.op('pool', lambda e: e.affine_select(ident[:], tril[:], [[-1, 128]], ALU.is_ge, 0.0, base=0,
                                           channel_multiplier=1), [tril.k()], [ident.k()])
    P.op('pool', lambda e: e.iota(iot_i[:], [[1, 128]], base=0, channel_multiplier=0), [], [iot_i.k()])
    P.cp(iot[:], iot_i[:], [iot_i.k()], [iot.k()])
    P.dma('sp', rc[:], rcc.ap, [rcc.k()], [rc.k()], chan='rc')
    for l in range(L):
        P.dma('sp', spm[l][:], smallp.ap[l], [smallp.k()], [spm[l].k()], chan=f'spm{l}')
    for ti, (c0, n) in enumerate(TILES):
        P.dma('pool', xb[:, :, c0:c0 + n], xT0.ap[:, :, c0:c0 + n], [xT0.k()], [xb.k(ti)], chan='xb')
    P.release(base_mark)

    def sincos(ang, cs, sn, scr, n, keys_r):
        (s0, k0), (s1, k1), (s2, k2), (s3, k3), (si, ki) = scr
        P.ts(s0, ang, 1.0 / (2 * math.pi), None, ALU.mult, None, keys_r, [k0])
        P.cp(si, s0, [k0], [ki])
        P.cp(s0, si, [ki], [k0])
        P.stt(s1, s0, -2.0 * math.pi, ang, ALU.mult, ALU.add, [k0] + keys_r, [k1])
        P.act(s2, s1, AF.Sin, [k1], [k2], scale=0.5)
        P.act(s3, s1, AF.Sin, [k1], [k3], scale=0.25)
        P.tt(s0, s3, s3, ALU.mult, [k3], [k0])
        P.ts(s0, s0, -2.0, 1.0, ALU.mult, ALU.add, [k0], [k0])
        return (s0, k0), (s1, k1), (s2, k2), (s3, k3)

    def sincos_finish(cs, sn, kcs, ksn, s0, k0, s2, k2, s3, k3):
        P.stt(sn, s2, 2.0, s0, ALU.mult, ALU.mult, [k2, k0], [ksn])
        P.tt(s3, s2, s2, ALU.mult, [k2], [k3])
        P.ts(cs, s3, -2.0, 1.0, ALU.mult, ALU.add, [k3], [kcs])

    def resid_ln(l, which, ti, psum_dc, src, dst, goff, boff, extra_r, ln_m):
        c0, n = TILES[ti]
        xt = ln_m['xt'][ti % 2]
        P.dma('sp', xt[:, :, 0:n], src.ap[:, :, c0:c0 + n], [src.k(ti)], [xt.k()], chan=f'xt{ti % 2}')
        for dc in range(KC):
            pap, pk = psum_dc(dc)
            P.stt(xt[:, dc, 0:n], xt[:, dc, 0:n], ALPHA, pap, ALU.mult, ALU.add, [xt.k(), pk], [xt.k()])
        xbt, sqt = ln_m['xbt'], ln_m['sqt']
        P.cp(xbt[:, :, 0:n], xt[:, :, 0:n], [xt.k()], [xbt.k()], eng='act')
        P.act(sqt[:, :, 0:n], xt[:, :, 0:n], AF.Square, [xt.k()], [sqt.k()])
        s1, s2 = bank(6), bank(7)
        for dc in range(KC):
            P.mm(s1[:, 0:n], ones_bf[:], xbt[:, dc, 0:n], dc == 0, dc == KC - 1, [ones_bf.k(), xbt.k()], [s1.k()])
        for dc in range(KC):
            P.mm(s2[:, 0:n], ones_bf[:], sqt[:, dc, 0:n], dc == 0, dc == KC - 1, [ones_bf.k(), sqt.k()], [s2.k()])
        mean, var, rstd = ln_m['mean'], ln_m['var'], ln_m['rstd']
        P.op('act', lambda e: e.mul(mean[:, 0:n], s1[:, 0:n], 1.0 / D), [s1.k()], [mean.k()])
        P.tt(var[:, 0:n], mean[:, 0:n], mean[:, 0:n], ALU.mult, [mean.k()], [var.k()])
        P.stt(var[:, 0:n], s2[:, 0:n], 1.0 / D, var[:, 0:n], ALU.mult, ALU.subtract, [s2.k(), var.k()], [var.k()])
        P.ts(var[:, 0:n], var[:, 0:n], EPS, None, ALU.add, None, [var.k()], [var.k()])
        P.act(var[:, 0:n], var[:, 0:n], AF.Sqrt, [var.k()], [var.k()])
        P.op('dve', lambda e: e.reciprocal(rstd[:, 0:n], var[:, 0:n]), [var.k()], [rstd.k()])
        for dc in range(KC):
            P.tt(xt[:, dc, 0:n], xt[:, dc, 0:n], mean[:, 0:n], ALU.subtract, [xt.k(), mean.k()], [xt.k()])
            P.tt(xt[:, dc, 0:n], xt[:, dc, 0:n], rstd[:, 0:n], ALU.mult, [xt.k(), rstd.k()], [xt.k()])
            P.act(xt[:, dc, 0:n], xt[:, dc, 0:n], AF.Identity, [xt.k(), spm[l].k()], [xt.k()],
                  bias=spm[l][:, boff + dc:boff + dc + 1], scale=spm[l][:, goff + dc:goff + dc + 1])
        P.cp(xb[:, :, c0:c0 + n], xt[:, :, 0:n], [xt.k()] + extra_r, [xb.k(ti)], eng='act')
        P.dma('sp', dst.ap[:, :, c0:c0 + n], xt[:, :, 0:n], [xt.k()], [dst.k(ti)], chan=f'xt{ti % 2}')

    def resid_ln_seq(l, tis, mmfn, src, dst, goff, boff, ln_m, mode, xb_mode):
        bctr = [0]
        pend = {}
        xbt, sqt = ln_m['xbt'], ln_m['sqt']
        mean, var, rstd = ln_m['mean'], ln_m['var'], ln_m['rstd']
        s1, s2 = bank(6), bank(7)

        def A1(ti, dc):
            pb = bank(bctr[0] % 6)
            bctr[0] += 1
            mmfn(ti, dc, pb)
            return pb

        def pre(ti, dcs):
            for dc in dcs:
                pend[(ti, dc)] = A1(ti, dc)

        def XT(ti):
            return ln_m['xt'][ti % 2]

        def stepA(ti):
            c0, n = TILES[ti]
            xt = XT(ti)
            P.dma('sp', xt[:, :, 0:n], src.ap[:, :, c0:c0 + n], [src.k(ti)], [xt.k()], chan=f'xt{ti % 2}')
            for dc in range(KC):
                pb = pend.pop((ti, dc)) if (ti, dc) in pend else A1(ti, dc)
                P.stt(xt[:, dc, 0:n], xt[:, dc, 0:n], ALPHA, pb[:, 0:n], ALU.mult, ALU.add, [xt.k(dc), pb.k()], [xt.k(dc)])
            P.cp(xbt[:, :, 0:n], xt[:, :, 0:n], [xt.k()], [xbt.k()], eng='act')
            P.act(sqt[:, :, 0:n], xt[:, :, 0:n], AF.Square, [xt.k()], [sqt.k()])

        def stats_mm(ti):
            c0, n = TILES[ti]
            for dc in range(KC):
                P.mm(s1[:, 0:n], ones_bf[:], xbt[:, dc, 0:n], dc == 0, dc == KC - 1, [ones_bf.k(), xbt.k()], [s1.k()])
            for dc in range(KC):
                P.mm(s2[:, 0:n], ones_bf[:], sqt[:, dc, 0:n], dc == 0, dc == KC - 1, [ones_bf.k(), sqt.k()], [s2.k()])

        def stats_ew(ti):
            c0, n = TILES[ti]
            P.op('act', lambda e, n=n: e.mul(mean[:, 0:n], s1[:, 0:n], 1.0 / D), [s1.k()], [mean.k()])
            P.tt(var[:, 0:n], mean[:, 0:n], mean[:, 0:n], ALU.mult, [mean.k()], [var.k()])
            P.stt(var[:, 0:n], s2[:, 0:n], 1.0 / D, var[:, 0:n], ALU.mult, ALU.subtract, [s2.k(), var.k()], [var.k()])
            P.ts(var[:, 0:n], var[:, 0:n], EPS, None, ALU.add, None, [var.k()], [var.k()])
            P.act(var[:, 0:n], var[:, 0:n], AF.Sqrt, [var.k()], [var.k()])
            P.op('dve', lambda e, n=n: e.reciprocal(rstd[:, 0:n], var[:, 0:n]), [var.k()], [rstd.k()])

        def norm(ti):
            c0, n = TILES[ti]
            xt = XT(ti)
            for dc in range(KC):
                P.tt(xt[:, dc, 0:n], xt[:, dc, 0:n], mean[:, 0:n], ALU.subtract, [xt.k(dc), mean.k()], [xt.k(dc)])
                P.tt(xt[:, dc, 0:n], xt[:, dc, 0:n], rstd[:, 0:n], ALU.mult, [xt.k(dc), rstd.k()], [xt.k(dc)])
                P.act(xt[:, dc, 0:n], xt[:, dc, 0:n], AF.Identity, [xt.k(dc), spm[l].k()], [xt.k(dc)],
                      bias=spm[l][:, boff + dc:boff + dc + 1], scale=spm[l][:, goff + dc:goff + dc + 1])
            if xb_mode == 'act':
                P.cp(xb[:, :, c0:c0 + n], xt[:, :, 0:n], [xt.k()], [xb.k(ti)], eng='act')
            P.dma('sp', dst.ap[:, :, c0:c0 + n], xt[:, :, 0:n], [xt.k()], [dst.k(ti)], chan=f'xt{ti % 2}')
            if xb_mode == 'dma':
                P.dma('pool', xb[:, :, c0:c0 + n], dst.ap[:, :, c0:c0 + n], [dst.k(ti)], [xb.k(ti)], chan='xb')

        nt = len(tis)
        if mode == 'p4':
            pre(tis[0], range(6))
            for i, ti in enumerate(tis):
                stepA(ti)
                if i + 1 < nt:
                    pre(tis[i + 1], range(2))
                stats_mm(ti)
                if i + 1 < nt:
                    pre(tis[i + 1], range(2, 6))
                if i > 0:
                    norm(tis[i - 1])
                stats_ew(ti)
            last_ti = tis[-1]
            return lambda: norm(last_ti)
        else:
            pre(tis[0], range(6))
            for i, ti in enumerate(tis):
                stepA(ti)
                if i + 1 < nt:
                    pre(tis[i + 1], range(2))
                stats_mm(ti)
                if i + 1 < nt:
                    pre(tis[i + 1], range(2, 6))
                stats_ew(ti)
                norm(ti)

    KSTOP = int(os.environ.get('KSTOP', '99'))
    stage = [0]

    def stop_here():
        stage[0] += 1
        return stage[0] >= KSTOP
    for l in range(L):
        if stage[0] >= KSTOP:
            break
        src0 = xT0 if l == 0 else res[1]
        mid = res[0] if l == 0 else res[2]
        dst2 = res[1] if l == 0 else yT
        sp = spm[l]

        ya = P.alloc("ya", [128, 4, NT], BF16)
        m_ssm = P.mark()
        win_s = P.alloc("win_s", [128, KC, 512], BF16)
        wglu = P.alloc("wglu", [128, 4, 512], BF16)
        lb1 = P.alloc("lhsT_B1", [128, 2, 16, 128], BF16)
        lb0 = P.alloc("lhsT_B0", [128, 2, 16, 128], BF16)
        lc = P.alloc("lhsT_C", [128, 2, 16, 128], BF16)
        lg = P.alloc("lhsT_G", [128, 2, 16, 128], BF16)
        k0m = P.alloc("k0m", [128, 4, 128], BF16)
        Ec = P.alloc("Ec", [128, 16, 64], F32)
        Es = P.alloc("Es", [128, 16, 64], F32)
        DkP = P.alloc("DkP", [128, 16, 64], F32)
        DkS = P.alloc("DkS", [128, 16, 64], F32)
        pw = P.alloc("pw", [128, 32, 16], F32)
        pwi = P.alloc("pwi", [128, 16], I32)
        hp = P.alloc("hp", [128, 2, 16], F32)
        h0t = P.alloc("h0t", [128, 2, 16, 16], F32)
        hso = P.alloc("hso", [128, 2, 16, 16], F32)
        sc = [P.alloc(f"sc{i}", [128, 1024], F32) for i in range(8)]
        sci = P.alloc("sci", [128, 1024], I32)
        P.dma('pool', win_s[:], w_in.ap[l, :, :, 0:O1], [w_in.k()], [win_s.k()], chan='win_s')
        P.dma('pool', wglu[:], w_glu.ap[l], [w_glu.k()], [wglu.k()], chan='wglu')
        P.dma('sp', h0t[:], h0.ap[l], [h0.k()], [h0t.k()], chan='h0t')

        def S(i):
            return pw[:, i, :], pw.k(i)
        are = sp[:, SP_ARE:SP_ARE + 16]
        aim = sp[:, SP_AIM:SP_AIM + 16]
        lsp = sp[:, SP_LS:SP_LS + 16]
        (dt_, kdt), (x1, kx1), (mag, kmag), (ang, kang) = S(0), S(1), S(2), S(3)
        P.act(dt_, lsp, AF.Exp, [sp.k()], [kdt])
        P.tt(x1, are, dt_, ALU.mult, [sp.k(), kdt], [kx1])
        P.act(mag, x1, AF.Exp, [kx1], [kmag])
        P.tt(ang, aim, dt_, ALU.mult, [sp.k(), kdt], [kang])
        r0, r1, r2, r3 = sincos(ang, None, None, [S(4), S(5), S(6), S(7), (pwi[:], pwi.k())], 16, [kang])
        (cs, kcs), (sn, ksn) = S(8), S(9)
        angp, kangp = r1
        sincos_finish(cs, sn, kcs, ksn, r0[0], r0[1], r2[0], r2[1], r3[0], r3[1])
        (abr, kabr), (abi, kabi) = S(10), S(11)
        P.tt(abr, mag, cs, ALU.mult, [kmag, kcs], [kabr])
        P.tt(abi, mag, sn, ALU.mult, [kmag, ksn], [kabi])
        (den, kden), (tq, ktq), (nr, knr), (kr, kkr), (ki_, kki) = S(12), S(13), S(14), S(15), S(16)
        (ta, kta), (tb, ktb) = S(17), S(18)
        P.tt(den, are, are, ALU.mult, [sp.k()], [kden])
        P.tt(tq, aim, aim, ALU.mult, [sp.k()], [ktq])
        P.tt(den, den, tq, ALU.add, [kden, ktq], [kden])
        P.op('dve', lambda e: e.reciprocal(den, den), [kden], [kden])
        P.ts(nr, abr, -1.0, None, ALU.add, None, [kabr], [knr])
        P.tt(ta, nr, are, ALU.mult, [knr, sp.k()], [kta])
        P.tt(tb, abi, aim, ALU.mult, [kabi, sp.k()], [ktb])
        P.tt(ta, ta, tb, ALU.add, [kta, ktb], [kta])
        P.tt(kr, ta, den, ALU.mult, [kta, kden], [kkr])
        P.tt(ta, abi, are, ALU.mult, [kabi, sp.k()], [kta])
        P.tt(tb, nr, aim, ALU.mult, [knr, sp.k()], [ktb])
        P.tt(ta, ta, tb, ALU.subtract, [kta, ktb], [kta])
        P.tt(ki_, ta, den, ALU.mult, [kta, kden], [kki])
        (akr, kakr), (aki, kaki), (a2r, ka2r), (a2i, ka2i), (mag2, kmag2), (ang2, kang2) = S(19), S(20), S(21), S(22), S(23), S(24)
        P.tt(ta, abr, kr, ALU.mult, [kabr, kkr], [kta])
        P.tt(tb, abi, ki_, ALU.mult, [kabi, kki], [ktb])
        P.tt(akr, ta, tb, ALU.subtract, [kta, ktb], [kakr])
        P.tt(ta, abr, ki_, ALU.mult, [kabr, kki], [kta])
        P.tt(tb, abi, kr, ALU.mult, [kabi, kkr], [ktb])
        P.tt(aki, ta, tb, ALU.add, [kta, ktb], [kaki])
        P.tt(ta, abr, abr, ALU.mult, [kabr], [kta])
        P.tt(tb, abi, abi, ALU.mult, [kabi], [ktb])
        P.tt(a2r, ta, tb, ALU.subtract, [kta, ktb], [ka2r])
        P.stt(a2i, abr, 2.0, abi, ALU.mult, ALU.mult, [kabr, kabi], [ka2i])
        P.tt(mag2, mag, mag, ALU.mult, [kmag], [kmag2])
        P.ts(ang2, angp, 2.0, None, ALU.mult, None, [kangp], [kang2])
        for hfs in range(2):
            H0 = hfs * 8
            bre, bim = sc[3], sc[4]
            P.dma('sp', bre[:], bpad.ap[l, :, 0, H0:H0 + 8].rearrange("p a b -> p (a b)"), [bpad.k()], [bre.k()], chan='bre')
            P.dma('sp', bim[:], bpad.ap[l, :, 1, H0:H0 + 8].rearrange("p a b -> p (a b)"), [bpad.k()], [bim.k()], chan='bim')
            for (ks_r, kk_r, ks_i, kk_i, lbx) in ((kr, kkr, ki_, kki, lb1), (akr, kakr, aki, kaki, lb0)):
                krF, kiF = sc[0], sc[1]
                for (ksrc, kk, dstF) in ((ks_r, kk_r, krF), (ks_i, kk_i, kiF)):
                    dg = sc[2]
                    dg3 = dg[:].rearrange("p (a b) -> p a b", a=8)
                    P.tt(dg3, ident[:].unsqueeze(1).to_broadcast([128, 8, 128]),
                         ksrc[:, H0:H0 + 8].unsqueeze(2).to_broadcast([128, 8, 128]), ALU.mult, [ident.k(), kk], [dg.k()])
                    for j in range(2):
                        P.mm(bank(j)[:], ones_f[:], dg[:, j * 512:(j + 1) * 512], True, True, [ones_f.k(), dg.k()], [bank(j).k()])
                        P.cp(dstF[:, j * 512:(j + 1) * 512], bank(j)[:], [bank(j).k()], [dstF.k()], eng='act')
                t5, t6 = sc[5], sc[6]
                lb_re = lbx[:, 0, H0:H0 + 8].rearrange("p a b -> p (a b)")
                lb_im = lbx[:, 1, H0:H0 + 8].rearrange("p a b -> p (a b)")
                P.tt(t5[:], krF[:], bre[:], ALU.mult, [krF.k(), bre.k()], [t5.k()])
                P.tt(t6[:], kiF[:], bim[:], ALU.mult, [kiF.k(), bim.k()], [t6.k()])
                P.tt(lb_re, t5[:], t6[:], ALU.subtract, [t5.k(), t6.k()], [lbx.k((0, hfs))])
                P.tt(t5[:], krF[:], bim[:], ALU.mult, [krF.k(), bim.k()], [t5.k()])
                P.tt(t6[:], kiF[:], bre[:], ALU.mult, [kiF.k(), bre.k()], [t6.k()])
                P.tt(lb_im, t5[:], t6[:], ALU.add, [t5.k(), t6.k()], [lbx.k((1, hfs))])
            cre, cim = sc[3], sc[4]
            P.dma('sp', cre[:], cpad.ap[l, :, 0, H0:H0 + 8].rearrange("p a b -> p (a b)"), [cpad.k()], [cre.k()], chan='bre')
            P.dma('sp', cim[:], cpad.ap[l, :, 1, H0:H0 + 8].rearrange("p a b -> p (a b)"), [cpad.k()], [cim.k()], chan='bim')
            P.cp(lc[:, 0, H0:H0 + 8].rearrange("p a b -> p (a b)"), cre[:], [cre.k()], [lc.k((0, hfs))], eng='act')
            P.op('act', lambda e, H0=H0, cim=cim: e.mul(lc[:, 1, H0:H0 + 8].rearrange("p a b -> p (a b)"), cim[:], -1.0),
                 [cim.k()], [lc.k((1, hfs))])
            arb = abr[:, H0:H0 + 8].unsqueeze(2).to_broadcast([128, 8, 128])
            aib = abi[:, H0:H0 + 8].unsqueeze(2).to_broadcast([128, 8, 128])
            t5, t6 = sc[5], sc[6]
            t53 = t5[:].rearrange("p (a b) -> p a b", a=8)
            t63 = t6[:].rearrange("p (a b) -> p a b", a=8)
            cre3 = cre[:].rearrange("p (a b) -> p a b", a=8)
            cim3 = cim[:].rearrange("p (a b) -> p a b", a=8)
            P.tt(t53, cre3, arb, ALU.mult, [cre.k(), kabr], [t5.k()])
            P.tt(t63, cim3, aib, ALU.mult, [cim.k(), kabi], [t6.k()])
            P.tt(lg[:, 0, H0:H0 + 8].rearrange("p a b -> p (a b)"), t5[:], t6[:], ALU.subtract, [t5.k(), t6.k()], [lg.k((0, hfs))])
            P.tt(t53, cre3, aib, ALU.mult, [cre.k(), kabi], [t5.k()])
            P.tt(t63, cim3, arb, ALU.mult, [cim.k(), kabr], [t6.k()])
            P.stt(lg[:, 1, H0:H0 + 8].rearrange("p a b -> p (a b)"), t5[:], -1.0, t6[:], ALU.mult, ALU.subtract,
                  [t5.k(), t6.k()], [lg.k((1, hfs))])
            b2r, b2i = sc[0], sc[1]
            P.dma('sp', b2r[:], bpad2.ap[l, :, 0, H0:H0 + 8].rearrange("p a b -> p (a b)"), [bpad2.k()], [b2r.k()], chan='b2r')
            P.dma('sp', b2i[:], bpad2.ap[l, :, 1, H0:H0 + 8].rearrange("p a b -> p (a b)"), [bpad2.k()], [b2i.k()], chan='b2i')
            krb = kr[:, H0:H0 + 8].unsqueeze(2).to_broadcast([128, 8, 128])
            kib = ki_[:, H0:H0 + 8].unsqueeze(2).to_broadcast([128, 8, 128])
            b2r3 = b2r[:].rearrange("p (a b) -> p a b", a=8)
            b2i3 = b2i[:].rearrange("p (a b) -> p a b", a=8)
            xr, xi = sc[2], sc[7]
            xrb = xr[:].bitcast(BF16)[:, 0:1024].rearrange("p (a b) -> p a b", a=8)
            xib = xi[:].bitcast(BF16)[:, 0:1024].rearrange("p (a b) -> p a b", a=8)
            P.tt(t53, b2r3, krb, ALU.mult, [b2r.k(), kkr], [t5.k()])
            P.tt(t63, b2i3, kib, ALU.mult, [b2i.k(), kki], [t6.k()])
            P.tt(xrb, t53, t63, ALU.subtract, [t5.k(), t6.k()], [xr.k()])
            P.tt(t53, b2i3, krb, ALU.mult, [b2i.k(), kkr], [t5.k()])
            P.tt(t63, b2r3, kib, ALU.mult, [b2r.k(), kki], [t6.k()])
            P.tt(xib, t53, t63, ALU.add, [t5.k(), t6.k()], [xi.k()])
            for cq in range(2):
                cc = 2 * hfs + cq
                pk = bank(2 + cq)
                idx = 0
                for gq in range(4):
                    gp = cc * 4 + gq
                    gl = gp - H0
                    for (xb_, ri) in ((xrb, 0), (xib, 1)):
                        P.mm(pk[:, 0:128], xb_[:, gl, :], lc[:, ri, gp, :], idx == 0, idx == 7,
                             [xr.k(), xi.k(), lc.k()], [pk.k()])
                        idx += 1
                P.cp(k0m[:, cc, :], pk[:, 0:128], [pk.k()], [k0m.k(cc)], eng='act')
            angt = sc[3]
            angt3 = angt[:, 0:512].rearrange("p (a b) -> p a b", a=8)
            P.tt(angt3, ang2[:, H0:H0 + 8].unsqueeze(2).to_broadcast([128, 8, 64]),
                 iot[:, 0:64].unsqueeze(1).to_broadcast([128, 8, 64]), ALU.mult, [kang2, iot.k()], [angt.k()])
            q0, q1, q2, q3 = sincos(angt[:, 0:512], None, None,
                                    [(sc[4][:, 0:512], sc[4].k()), (sc[5][:, 0:512], sc[5].k()), (sc[6][:, 0:512], sc[6].k()),
                                     (sc[0][:, 0:512], sc[0].k()), (sci[:, 0:512], sci.k())], 512, [angt.k()])
            sincos_finish(Ec[:, H0:H0 + 8].rearrange("p a b -> p (a b)"), Es[:, H0:H0 + 8].rearrange("p a b -> p (a b)"),
                          Ec.k(hfs), Es.k(hfs), q0[0], q0[1], q2[0], q2[1], q3[0], q3[1])
        P.cp(DkP[:], mag2.unsqueeze(2).to_broadcast([128, 16, 64]), [kmag2], [DkP.k()])
        P.memset(DkP[:, :, 0:1], 0.0, [DkP.k()])
        P.cp(DkS[:], mag2.unsqueeze(2).to_broadcast([128, 16, 64]), [kmag2], [DkS.k()])
        P.memset(DkS[:].rearrange("p a (s t) -> p a s t", t=4)[:, :, :, 0:1], 0.0, [DkS.k()])

        pu, py, pg_ = bank(0), bank(5), bank(6)
        pbr = P.psum_buf("pbr", 512, 512, [128, 8, 64])
        pbi = P.psum_buf("pbi", 1536, 512, [128, 8, 64])
        ubl = [P.alloc(f"ub{i}", [128, 4, 2, 64], BF16) for i in range(2)]
        ufl = [P.alloc(f"uf{i}", [128, 4, 2, 64], F32) for i in range(2)]
        brl = [P.alloc(f"br{i}", [128, 8, 64], F32) for i in range(2)]
        bil = [P.alloc(f"bi{i}", [128, 8, 64], F32) for i in range(2)]
        hb = P.alloc("hb", [128, 2, 8, 80], BF16)
        zf = P.alloc("zf", [128, 4, 128], F32)
        zb = P.alloc("zb", [128, 4, 128], BF16)
        sg = P.alloc("sg", [128, 4, 128], F32)
        cw = P.alloc("cw", [128, 4, 8, 16], F32)
        t1, t2, t3, t4, btr, bti, gr, gi = [s_[:, 0:512].rearrange("p (a b) -> p a b", a=8) for s_ in sc]
        kt1, kt2, kt3, kt4, kbtr, kbti, kgr, kgi = [s_.k() for s_ in sc]

        def blk(b):
            c0 = b * 128
            return c0, (b == NB - 1), tile_of(c0)

        def views(b, hf):
            c0, smp, tix = blk(b)
            G0 = hf * 8
            ec, es = Ec[:, G0:G0 + 8, :], Es[:, G0:G0 + 8, :]
            if smp:
                ec = Ec[:, G0:G0 + 8, 0:4].unsqueeze(2).to_broadcast([128, 8, 16, 4])
                es = Es[:, G0:G0 + 8, 0:4].unsqueeze(2).to_broadcast([128, 8, 16, 4])

                def V(a):
                    return a.rearrange("p a (s t) -> p a s t", t=4)
            else:
                def V(a):
                    return a
            return G0, ec, es, V

        def stage_U(b):
            c0, smp, tix = blk(b)
            ub, uf = ubl[b % 2], ufl[b % 2]
            for cc in range(4):
                for k in range(KC):
                    P.mm(pu[:, cc * 128:(cc + 1) * 128], win_s[:, k, cc * 128:(cc + 1) * 128], xb[:, k, c0:c0 + 128],
                         k == 0, k == KC - 1, [win_s.k(), xb.k(tix)], [pu.k()])
            pu4 = pu[:].rearrange("p (c j e) -> p c e j", c=4, e=2)
            P.cp(ub[:], pu4, [pu.k()], [ub.k()], eng='act')
            P.cp(uf[:], pu4, [pu.k()], [uf.k()], eng='act')

        def stage_BU(b, hf):
            ub = ubl[b % 2]
            G0 = hf * 8
            s_ = (2 * b + hf) % 2
            for gl in range(8):
                gp = G0 + gl
                for (pbx, ri) in ((pbr, 0), (pbi, 1)):
                    P.mm(pbx[:, gl, :], lb0[:, ri, gp, :], ub[:, gp // 4, 0, :], True, False, [lb0.k(), ub.k()], [pbx.k()])
                    P.mm(pbx[:, gl, :], lb1[:, ri, gp, :], ub[:, gp // 4, 1, :], False, True, [lb1.k(), ub.k()], [pbx.k()])
            P.cp(brl[s_][:], pbr[:], [pbr.k()], [brl[s_].k()], eng='act')
            P.cp(bil[s_][:], pbi[:], [pbi.k()], [bil[s_].k()], eng='act')

        def stage_ROT(b, hf):
            c0, smp, tix = blk(b)
            G0, ec, es, V = views(b, hf)
            s_ = (2 * b + hf) % 2
            br_, bi_ = brl[s_], bil[s_]
            P.tt(V(t1), V(br_[:]), ec, ALU.mult, [br_.k(), Ec.k()], [kt1])
            P.tt(V(t2), V(bi_[:]), es, ALU.mult, [bi_.k(), Es.k()], [kt2])
            P.tt(V(t3), V(bi_[:]), ec, ALU.mult, [bi_.k(), Ec.k()], [kt3])
            P.tt(V(t4), V(br_[:]), es, ALU.mult, [br_.k(), Es.k()], [kt4])
            P.tt(btr, t1, t2, ALU.add, [kt1, kt2], [kbtr])
            P.tt(bti, t3, t4, ALU.subtract, [kt3, kt4], [kbti])
            if smp:
                hr, hi_ = h0t[:, 0, G0:G0 + 8, :], h0t[:, 1, G0:G0 + 8, :]
                khp = h0t.k()
                hb5 = hb[:].rearrange("p r a (s t) -> p r a s t", t=5)
                P.cp(hb5[:, :, :, :, 0], h0t[:, :, G0:G0 + 8, :], [khp], [hb.k('c')])
            elif b > 0:
                hr, hi_ = hp[:, 0, G0:G0 + 8], hp[:, 1, G0:G0 + 8]
                khp = hp.k(hf)
                P.cp(hb[:, :, :, 0], hp[:, :, G0:G0 + 8], [khp], [hb.k('c')])
            else:
                P.memset(hb[:, :, :, 0:1], 0.0, [hb.k('c')])
            if smp or b > 0:
                if smp:
                    ar = a2r[:, G0:G0 + 8].unsqueeze(2).to_broadcast([128, 8, 16])
                    ai = a2i[:, G0:G0 + 8].unsqueeze(2).to_broadcast([128, 8, 16])
                    c1, c2, c3, c4 = cw[:, 0], cw[:, 1], cw[:, 2], cw[:, 3]
                    b0r = btr.rearrange("p a (s t) -> p a s t", t=4)[:, :, :, 0]
                    b0i = bti.rearrange("p a (s t) -> p a s t", t=4)[:, :, :, 0]
                else:
                    ar, ai = a2r[:, G0:G0 + 8], a2i[:, G0:G0 + 8]
                    c1, c2, c3, c4 = cw[:, 0, :, 0], cw[:, 1, :, 0], cw[:, 2, :, 0], cw[:, 3, :, 0]
                    b0r, b0i = btr[:, :, 0], bti[:, :, 0]
                P.tt(c1, ar, hr, ALU.mult, [ka2r, khp], [cw.k(0)])
                P.tt(c2, ai, hi_, ALU.mult, [ka2i, khp], [cw.k(1)])
                P.tt(c1, c1, c2, ALU.subtract, [cw.k(0), cw.k(1)], [cw.k(0)])
                P.tt(b0r, b0r, c1, ALU.add, [kbtr, cw.k(0)], [kbtr])
                P.tt(c3, ar, hi_, ALU.mult, [ka2r, khp], [cw.k(2)])
                P.tt(c4, ai, hr, ALU.mult, [ka2i, khp], [cw.k(3)])
                P.tt(c3, c3, c4, ALU.add, [cw.k(2), cw.k(3)], [cw.k(2)])
                P.tt(b0i, b0i, c3, ALU.add, [kbti, cw.k(2)], [kbti])
            Dk = DkS if smp else DkP
            dk = Dk[:, G0:G0 + 8, :].rearrange("p a b -> p (a b)")
            P.op('dve', lambda e, dk=dk: e.tensor_tensor_scan(sc[6][:, 0:512], dk, sc[4][:, 0:512], 0.0, ALU.mult, ALU.add),
                 [Dk.k(), kbtr], [kgr])
            P.op('dve', lambda e, dk=dk: e.tensor_tensor_scan(sc[7][:, 0:512], dk, sc[5][:, 0:512], 0.0, ALU.mult, ALU.add),
                 [Dk.k(), kbti], [kgi])

        def stage_UNROT(b, hf):
            c0, smp, tix = blk(b)
            G0, ec, es, V = views(b, hf)
            P.tt(V(t1), V(gr), ec, ALU.mult, [kgr, Ec.k()], [kt1])
            P.tt(V(t2), V(gi), es, ALU.mult, [kgi, Es.k()], [kt2])
            P.tt(V(t3), V(gr), es, ALU.mult, [kgr, Es.k()], [kt3])
            P.tt(V(t4), V(gi), ec, ALU.mult, [kgi, Ec.k()], [kt4])
            if smp:
                hb5 = hb[:].rearrange("p r a (s t) -> p r a s t", t=5)
                P.tt(hb5[:, 0, :, :, 1:5], V(t1), V(t2), ALU.subtract, [kt1, kt2], [hb.k('d0')])
                P.tt(hb5[:, 1, :, :, 1:5], V(t3), V(t4), ALU.add, [kt3, kt4], [hb.k('d1')])
                P.tt(hso[:, 0, G0:G0 + 8, :], V(t1)[:, :, :, 3], V(t2)[:, :, :, 3], ALU.subtract, [kt1, kt2], [hso.k((0, hf))])
                P.tt(hso[:, 1, G0:G0 + 8, :], V(t3)[:, :, :, 3], V(t4)[:, :, :, 3], ALU.add, [kt3, kt4], [hso.k((1, hf))])
            else:
                P.tt(hb[:, 0, :, 1:65], t1, t2, ALU.subtract, [kt1, kt2], [hb.k('d0')])
                P.tt(hb[:, 1, :, 1:65], t3, t4, ALU.add, [kt3, kt4], [hb.k('d1')])
                P.tt(hp[:, 0, G0:G0 + 8], t1[:, :, 63], t2[:, :, 63], ALU.subtract, [kt1, kt2], [hp.k(hf)])
                P.tt(hp[:, 1, G0:G0 + 8], t3[:, :, 63], t4[:, :, 63], ALU.add, [kt3, kt4], [hp.k(hf)])

        def stage_CY(b, hf):
            c0, smp, tix = blk(b)
            ub = ubl[b % 2]
            G0 = hf * 8
            hb5 = hb[:].rearrange("p r a (s t) -> p r a s t", t=5)
            for cc in (2 * hf, 2 * hf + 1):
                for eo in range(2):
                    out = py[:, cc * 128 + eo * 64:cc * 128 + eo * 64 + 64]
                    idx = 0
                    nmm = 9 if eo == 0 else 8
                    for gq in range(4):
                        gp = cc * 4 + gq
                        gl = gp - G0
                        for ri in range(2):
                            if smp:
                                rhs = hb5[:, ri, gl, :, eo:eo + 4]
                            else:
                                rhs = hb[:, ri, gl, eo:eo + 64]
                            lh = lg if eo == 0 else lc
                            P.mm(out, lh[:, ri, gp, :], rhs, idx == 0, idx == nmm - 1, [lh.k(), hb.k()], [py.k()])
                            idx += 1
                    if eo == 0:
                        P.mm(out, k0m[:, cc, :], ub[:, cc, 0, :], False, True, [k0m.k(), ub.k()], [py.k()])

        def tailA(b):
            uf = ufl[b % 2]
            for cc in range(4):
                P.stt(zf[:, cc, :], uf[:, cc].rearrange("p e j -> p (e j)"), sp[:, SP_D + cc:SP_D + cc + 1],
                      py[:, cc * 128:(cc + 1) * 128], ALU.mult, ALU.add, [uf.k(), sp.k(), py.k()], [zf.k()])
            P.act(zf[:], zf[:], AF.Gelu_apprx_tanh, [zf.k()], [zf.k()])
            P.cp(zb[:], zf[:], [zf.k()], [zb.k()], eng='act')
            for oc in range(4):
                for k in range(4):
                    P.mm(pg_[:, oc * 128:(oc + 1) * 128], wglu[:, k, oc * 128:(oc + 1) * 128], zb[:, k, :],
                         k == 0, k == 3, [wglu.k(), zb.k()], [pg_.k()])
            for oc in range(4):
                P.act(sg[:, oc, :], pg_[:, oc * 128:(oc + 1) * 128], AF.Sigmoid, [pg_.k(), sp.k()], [sg.k()],
                      bias=sp[:, SP_BGLU + oc:SP_BGLU + oc + 1], scale=1.0)

        def tailB(b):
            c0, smp, tix = blk(b)
            yav = ya[:, :, c0:c0 + 128].rearrange("p c (j e) -> p c e j", e=2)
            P.tt(yav, zf[:].rearrange("p c (e j) -> p c e j", e=2), sg[:].rearrange("p c (e j) -> p c e j", e=2),
                 ALU.mult, [zf.k(), sg.k()], [ya.k(tix)])
            if b == NB - 2:
                P.dma('sp', o_ssm_p.ap[l], hp[:], [hp.k()], [o_ssm_p.k(l)], chan='hp')

        stage_U(0)
        stage_BU(0, 0)
        for b in range(NB):
            stage_ROT(b, 0)
            stage_BU(b, 1)
            stage_UNROT(b, 0)
            stage_CY(b, 0)
            if b > 0:
                tailB(b - 1)
            stage_ROT(b, 1)
            if b + 1 < NB:
                stage_U(b + 1)
                stage_BU(b + 1, 0)
            stage_UNROT(b, 1)
            stage_CY(b, 1)
            tailA(b)
        tailB(NB - 1)
        P.dma('sp', o_ssm_s.ap[l], hso[:], [hso.k()], [o_ssm_s.k(l)], chan='hso')
        P.release(m_ssm)
        if stop_here():
            break

        yb = P.alloc("yb", [128, 2, NT], BF16)
        m_g = P.mark()
        win_g = P.alloc("win_g", [128, KC, 512], BF16)
        wt_f = P.alloc("wt_f", [128, 2, 4, 128], F32)
        wtm = P.alloc("wtm", [128, 2, 4, 128], BF16)
        bct = P.alloc("bct", [128, NBC], F32)
        vnzl = [P.alloc(f"vnz{i}", [128, 2, 2, 128], BF16) for i in range(2)]
        ugl = [P.alloc(f"ug{i}", [128, 2, 128], F32) for i in range(3)]
        vgl = [P.alloc(f"vg{i}", [128, 256], F32) for i in range(2)]
        vnl = [P.alloc(f"vn{i}", [128, 256], F32) for i in range(2)]
        stt6 = P.alloc("stt6", [128, 6], F32)
        mv = P.alloc("mv", [128, 2], F32)
        sq_ = P.alloc("sq_", [128, 1], F32)
        ts_ = P.alloc("ts_", [128, 2, 128], F32)
        P.dma('pool', win_g[:], w_in.ap[l, :, :, O1:O3], [w_in.k()], [win_g.k()], chan='win_g')
        P.dma('sp', wt_f[:], wts.ap[l], [wts.k()], [wt_f.k()], chan='wt_f')
        P.dma('sp', bct[:], bcp.ap[l], [bcp.k()], [bct.k()], chan='bct')
        P.tt(wtm[:].rearrange("p a h t -> p (a h) t"), wt_f[:].rearrange("p a h t -> p (a h) t"),
             tril[:].unsqueeze(1).to_broadcast([128, 8, 128]), ALU.mult, [wt_f.k(), tril.k()], [wtm.k()])
        for v_ in vnzl:
            P.memset(v_[:], 0.0, [v_.k()])
        pu2, pv, pss = bank(0), bank(1), bank(2)

        def gA(b):
            c0 = b * 128
            tix = tile_of(c0)
            ug, vg = ugl[b % 3], vgl[b % 2]
            for cc in range(2):
                for k in range(KC):
                    P.mm(pu2[:, cc * 128:(cc + 1) * 128], win_g[:, k, cc * 128:(cc + 1) * 128], xb[:, k, c0:c0 + 128],
                         k == 0, k == KC - 1, [win_g.k(), xb.k(tix)], [pu2.k()])
            P.act(ug[:].rearrange("p a b -> p (a b)"), pu2[:, 0:256], AF.Gelu_apprx_tanh, [pu2.k()], [ug.k()])
            for k in range(KC):
                P.mm(pv[:, 0:256], xb[:, k, c0:c0 + 128], win_g[:, k, 256:512], k == 0, k == KC - 1,
                     [win_g.k(), xb.k(tix)], [pv.k()])
            P.act(vg[:], pv[:, 0:256], AF.Gelu_apprx_tanh, [pv.k()], [vg.k()])

        def gB(b):
            smp = (b == NB - 1)
            vg, vn, vnz = vgl[b % 2], vnl[b % 2], vnzl[b % 2]
            P.op('dve', lambda e: e.bn_stats(stt6[:], vg[:]), [vg.k()], [stt6.k()])
            P.op('dve', lambda e: e.bn_aggr(mv[:], stt6[:]), [stt6.k()], [mv.k()])
            P.ts(sq_[:], mv[:, 1:2], EPS, None, ALU.add, None, [mv.k()], [sq_.k()])
            P.act(sq_[:], sq_[:], AF.Sqrt, [sq_.k()], [sq_.k()])
            P.op('dve', lambda e: e.reciprocal(sq_[:], sq_[:]), [sq_.k()], [sq_.k()])
            P.ts(vn[:], vg[:], mv[:, 0:1], sq_[:, 0:1], ALU.subtract, ALU.mult, [vg.k(), mv.k(), sq_.k()], [vn.k()])
            P.tt(vn[:], vn[:], bct[:, BC_LNG:BC_LNG + 256], ALU.mult, [vn.k(), bct.k()], [vn.k()])
            P.tt(vn[:], vn[:], bct[:, BC_LNB:BC_LNB + 256], ALU.add, [vn.k(), bct.k()], [vn.k()])
            if smp:
                P.dma('sp', o_v_s.ap[l], vn[:], [vn.k()], [o_v_s.k(l)], chan='vn')
            vn4 = vn[:].rearrange("p (a h d) -> p a h d", a=2, h=2)
            P.cp(vnz[:, :, 0, 0:64], vn4[:, :, 0, :], [vn.k()], [vnz.k()])
            P.cp(vnz[:, :, 1, 64:128], vn4[:, :, 1, :], [vn.k()], [vnz.k()])

        def gC(b):
            c0 = b * 128
            smp = (b == NB - 1)
            tix = tile_of(c0)
            ug, vnz = ugl[b % 3], vnzl[b % 2]
            wsel = 1 if smp else 0
            for pr in range(2):
                for hh in range(2):
                    P.mm(pss[:, pr * 128:(pr + 1) * 128], vnz[:, pr, hh, :], wtm[:, wsel, 2 * pr + hh, :],
                         hh == 0, hh == 1, [vnz.k(), wtm.k()], [pss.k()])
            bso = BC_BSS if smp else BC_BSP
            P.tt(ts_[:].rearrange("p a b -> p (a b)"), pss[:, 0:256], bct[:, bso:bso + 256], ALU.add,
                 [pss.k(), bct.k()], [ts_.k()])
            P.tt(yb[:, :, c0:c0 + 128], ts_[:], ug[:], ALU.mult, [ts_.k(), ug.k()], [yb.k(tix)])

        gA(0)
        for b in range(NB):
            if b + 1 < NB:
                gA(b + 1)
            gB(b)
            if b > 0:
                gC(b - 1)
        gC(NB - 1)
        P.release(m_g)
        if stop_here():
            break

        yc = P.alloc("yc", [128, 4, NT], BF16)
        m_p = P.mark()
        win_p = P.alloc("win_p", [128, KC, 512], BF16)
        wpl = P.alloc("wpl", [128, 4, 128], BF16)
        xppl = [P.alloc(f"xpp{i}", [128, 16 + TP], F32) for i in range(2)]
        xpsl = [P.alloc(f"xps{i}", [128, 16, 24], F32) for i in range(2)]
        sa = P.alloc("sa", [128, 16 + TP], F32)
        sb_ = P.alloc("sb_", [128, 16 + TP], F32)
        ssa = P.alloc("ssa", [128, 16, 24], F32)
        ssb = P.alloc("ssb", [128, 16, 24], F32)
        dTl = [P.alloc(f"dT{i}", [128, NT], BF16) for i in range(2)]
        d16 = P.alloc("d16", [128, 16], F32)
        P.dma('pool', win_p[:], w_in.ap[l, :, :, O3:O4], [w_in.k()], [win_p.k()], chan='win_p')
        P.dma('pool', wpl[:], w_pool.ap[l], [w_pool.k()], [wpl.k()], chan='wpl')
        for x_ in xppl:
            P.memset(x_[:, 0:16], 0.0, [x_.k('pad')])

        def pool1(g):
            xpp, xps = xppl[g % 2], xpsl[g % 2]
            P.memset(xps[:, :, 0:1], 0.0, [xps.k('pad')])
            P.dma('sp', xps[:, :, 1:16], poolpast.ap[l, :, g], [poolpast.k()], [xps.k('pad')], chan=f'xps{g % 2}')
            for ti, (c0, n) in enumerate(TILES):
                pb = bank(ti % 4)
                for k in range(KC):
                    P.mm(pb[:, 0:n], win_p[:, k, g * 128:(g + 1) * 128], xb[:, k, c0:c0 + n], k == 0, k == KC - 1,
                         [win_p.k(), xb.k(ti)], [pb.k()])
                if ti < 4:
                    P.cp(xpp[:, 16 + c0:16 + c0 + n], pb[:, 0:n], [pb.k()], [xpp.k('d')], eng='act')
                else:
                    P.cp(xps[:, :, 16:24], pb[:, 0:128].rearrange("p (s t) -> p s t", t=8), [pb.k()], [xps.k('d')], eng='act')
            P.dma('sp', o_pool_p.ap[l, :, g], xpp[:, 16 + TP - 15:16 + TP], [xpp.k()], [o_pool_p.k((l, g))], chan=f'xpp{g % 2}')
            P.dma('sp', o_pool_s.ap[l, :, g], xps[:, :, 9:24], [xps.k()], [o_pool_s.k((l, g))], chan=f'xpso{g % 2}')

        def pool2(g):
            win = 2 ** (g + 1)
            xpp, xps, dT = xppl[g % 2], xpsl[g % 2], dTl[g % 2]
            curp, curs = xpp, xps
            bufs_p, bufs_s = [sa, sb_], [ssa, ssb]
            for kk in range(g + 1):
                sh = 2 ** kk
                lo = 2 ** (kk + 1) - 1
                np_, ns_ = bufs_p[kk % 2], bufs_s[kk % 2]
                P.tt(np_[:, lo:16 + TP], curp[:, lo:16 + TP], curp[:, lo - sh:16 + TP - sh], ALU.add, [curp.k()], [np_.k()])
                P.tt(ns_[:, :, lo:24], curs[:, :, lo:24], curs[:, :, lo - sh:24 - sh], ALU.add, [curs.k()], [ns_.k()])
                curp, curs = np_, ns_
            P.stt(dT[:, 0:TP], curp[:, 16:16 + TP], 1.0 / win, xpp[:, 16:16 + TP], ALU.mult, ALU.subtract,
                  [curp.k(), xpp.k()], [dT.k()])
            P.stt(dT[:, TP:NT].rearrange("p (s t) -> p s t", t=8), curs[:, :, 16:24], 1.0 / win, xps[:, :, 16:24],
                  ALU.mult, ALU.subtract, [curs.k(), xps.k()], [dT.k()])
            P.tt(d16[:], curp[:, 16:32], rc[:, g, :], ALU.mult, [curp.k(), rc.k()], [d16.k()])
            P.tt(dT[:, 0:16], d16[:], xpp[:, 16:32], ALU.subtract, [d16.k(), xpp.k()], [dT.k()])
            for ti, (c0, n) in enumerate(TILES):
                pb = bank(4 + ti % 4)
                P.mm(pb[:, 0:n], wpl[:, g, :], dT[:, c0:c0 + n], True, True, [wpl.k(), dT.k()], [pb.k()])
                P.act(yc[:, g, c0:c0 + n], pb[:, 0:n], AF.Identity, [pb.k(), sp.k()], [yc.k(ti)],
                      scale=sp[:, SP_PSC + g:SP_PSC + g + 1])

        pool1(0)
        for g in range(4):
            if g + 1 < 4:
                pool1(g + 1)
            pool2(g)
        P.release(m_p)
        if stop_here():
            break

        mg = P.alloc("mg", [128, KC, NT], BF16)
        m_m = P.mark()
        wgt = [P.alloc(f"wgt{i}", [128, KC, 3, 128], BF16) for i in range(2)]
        wbt = [P.alloc(f"wbt{i}", [128, 10, 128], BF16) for i in range(2)]
        sgt = [P.alloc(f"sgt{i}", [128, 512], F32) for i in range(2)]
        macc = P.alloc("macc", [128, 512], F32)
        mtmp = P.alloc("mtmp", [128, 512], F32)
        ybr = [(ya, 0, 4), (yb, 4, 2), (yc, 6, 4)]
        step = 0
        for dc in range(KC):
            wg, wb = wgt[dc % 2], wbt[dc % 2]
            P.dma('pool', wg[:], w_in.ap[l, :, :, O4:INC].rearrange("p k (i n) -> p k i n", i=3)[:, :, :, dc * 128:(dc + 1) * 128],
                  [w_in.k()], [wg.k()], chan=f'wgt{dc % 2}')
            P.dma('pool', wb[:], w_br.ap[l, :, :, dc * 128:(dc + 1) * 128], [w_br.k()], [wb.k()], chan=f'wbt{dc % 2}')
            for ti, (c0, n) in enumerate(TILES):
                for i in range(3):
                    pgt, pbt = bank((2 * step) % 8), bank((2 * step + 1) % 8)
                    sgi = sgt[step % 2]
                    step += 1
                    for k in range(KC):
                        P.mm(pgt[:, 0:n], wg[:, k, i, :], xb[:, k, c0:c0 + n], k == 0, k == KC - 1, [wg.k(), xb.k(ti)], [pgt.k()])
                    ysrc, koff, nk = ybr[i]
                    for k in range(nk):
                        P.mm(pbt[:, 0:n], wb[:, koff + k, :], ysrc[:, k, c0:c0 + n], k == 0, k == nk - 1,
                             [wb.k(), ysrc.k(ti)], [pbt.k()])
                    P.act(sgi[:, 0:n], pgt[:, 0:n], AF.Sigmoid, [pgt.k(), sp.k()], [sgi.k()],
                          bias=sp[:, SP_BG + i * 8 + dc:SP_BG + i * 8 + dc + 1], scale=1.0)
                    if i == 0:
                        P.tt(macc[:, 0:n], pbt[:, 0:n], sgi[:, 0:n], ALU.mult, [pbt.k(), sgi.k()], [macc.k()])
                    elif i == 1:
                        P.tt(mtmp[:, 0:n], pbt[:, 0:n], sgi[:, 0:n], ALU.mult, [pbt.k(), sgi.k()], [mtmp.k()])
                        P.tt(macc[:, 0:n], macc[:, 0:n], mtmp[:, 0:n], ALU.add, [macc.k(), mtmp.k()], [macc.k()])
                    else:
                        P.tt(mtmp[:, 0:n], pbt[:, 0:n], sgi[:, 0:n], ALU.mult, [pbt.k(), sgi.k()], [mtmp.k()])
                        P.tt(mg[:, dc, c0:c0 + n], macc[:, 0:n], mtmp[:, 0:n], ALU.add, [macc.k(), mtmp.k()], [mg.k(ti)])
        P.release(m_m)
        if stop_here():
            break

        ln_m = dict(xt=[P.alloc(f"xt{i}", [128, KC, 512], F32) for i in range(2)],
                    xbt=P.alloc("xbt", [128, KC, 512], BF16), sqt=P.alloc("sqt", [128, KC, 512], BF16),
                    mean=P.alloc("mean", [128, 512], F32), var=P.alloc("var", [128, 512], F32),
                    rstd=P.alloc("rstd", [128, 512], F32))
        wo = P.alloc("wo", [128, KC, D], BF16)
        P.dma('pool', wo[:], w_o.ap[l], [w_o.k()], [wo.k()], chan='wo')
        def mm_wo(ti, dc, pb):
            c0, n = TILES[ti]
            for k in range(KC):
                P.mm(pb[:, 0:n], wo[:, k, dc * 128:(dc + 1) * 128], mg[:, k, c0:c0 + n], k == 0, k == KC - 1,
                     [wo.k(), mg.k(ti)], [pb.k()])
        resid_ln_seq(l, [0, 1, 2, 3, 4], mm_wo, src0, mid, SP_LN1G, SP_LN1B, ln_m, 'p4', 'act')()
        fin_hold = [None]
        P.release(base_mark)
        if stop_here():
            break

        ln_m = dict(xt=[P.alloc(f"xt{i}", [128, KC, 512], F32) for i in range(2)],
                    xbt=P.alloc("xbt", [128, KC, 512], BF16), sqt=P.alloc("sqt", [128, KC, 512], BF16),
                    mean=P.alloc("mean", [128, 512], F32), var=P.alloc("var", [128, 512], F32),
                    rstd=P.alloc("rstd", [128, 512], F32))
        HALVES = [[0, 1], [2, 3, 4]]
        wup = [P.alloc(f"wup{i}", [128, KC, 2, 128], BF16) for i in range(2)]
        wdn = [P.alloc(f"wdn{i}", [128, NPAIR, 128], BF16) for i in range(3)]
        gbuf = P.alloc("gbuf", [128, NPAIR, 1152], BF16)
        hfl = [[P.alloc(f"hf{j}{i}", [128, 2 + 1024], F32) for i in range(2)] for j in range(2)]
        hsl = [[P.alloc(f"hs{j}{i}", [128, 16, 10], F32) for i in range(2)] for j in range(2)]
        cy = [P.alloc(f"cy{j}", [128, 1152], F32) for j in range(2)]
        cyb = [P.alloc_at(f"cyb{j}", [128, 1152], F32, ln_m['xbt'].lo + j * 4608) for j in range(2)]
        assert cyb[1].hi <= ln_m['rstd'].hi
        cysets = [cy, cyb]
        hcar = P.alloc("hcar", [128, 44, 2], F32)
        cpast = P.alloc("cpast", [128, 44, 16, 2], F32)
        csout = P.alloc("csout", [128, 44, 16, 2], F32)
        csp = P.alloc("csp", [128, 44, 2], F32)
        P.dma('sp', cpast[:], convpast.ap[l], [convpast.k()], [cpast.k()], chan='cpast')
        P.memset(hcar[:], 0.0, [hcar.k()])
        ui = 0
        di = 0
        for hi, tl in enumerate(HALVES):
            ptl = [t for t in tl if t < 4]
            NP_ = 512 * len(ptl)
            has_s = 4 in tl
            NN = NP_ + (128 if has_s else 0)
            ubk = [0]

            def mm_evac_pieces(pi):
                nonlocal ui
                wu = wup[ui % 2]
                cn = f'wup{ui % 2}'
                ui += 1
                P.dma('pool', wu[:], w_up.ap[l].rearrange("p k (h n) -> p k h n", h=2)[:, :, :, pi * 128:(pi + 1) * 128],
                      [w_up.k()], [wu.k()], chan=cn)
                pieces = []
                for j in range(2):
                    hfb = hfl[j][pi % 2]
                    hs_ = hsl[j][pi % 2]
                    for q, ti in enumerate(tl):
                        def piece(j=j, q=q, ti=ti, hfb=hfb, hs_=hs_, wu=wu):
                            c0, n = TILES[ti]
                            pb = bank(ubk[0] % 8)
                            ubk[0] += 1
                            for k in range(KC):
                                P.mm(pb[:, 0:n], wu[:, k, j, :], xb[:, k, c0:c0 + n], k == 0, k == KC - 1, [wu.k(), xb.k(ti)], [pb.k()])
                            if ti < 4:
                                P.cp(hfb[:, 2 + q * 512:2 + q * 512 + n], pb[:, 0:n], [pb.k()], [hfb.k('d')], eng='act')
                            else:
                                P.cp(hs_[:, :, 2:10], pb[:, 0:128].rearrange("p (s t) -> p s t", t=8), [pb.k()], [hs_.k('d')], eng='act')
                        pieces.append(piece)
                return pieces

            def tail_pieces(pi):
                cys = cysets[pi % 2]
                prm = []
                for j, chn in ((0, pi), (1, NPAIR + pi)):
                    w0 = sp[:, SP_CW + chn * 3 + 0:SP_CW + chn * 3 + 1]
                    w1 = sp[:, SP_CW + chn * 3 + 1:SP_CW + chn * 3 + 2]
                    w2 = sp[:, SP_CW + chn * 3 + 2:SP_CW + chn * 3 + 3]
                    cb = sp[:, SP_CB + chn:SP_CB + chn + 1]
                    prm.append((j, chn, w0, w1, w2, cb, hfl[j][pi % 2], hsl[j][pi % 2], cys[j]))

                def t_carry():
                    for (j, chn, w0, w1, w2, cb, hfb, hs_, cyj) in prm:
                        P.cp(hfb[:, 0:2], hcar[:, chn, :], [hcar.k(chn)], [hfb.k('c')])
                        if has_s:
                            P.cp(hs_[:, :, 0:2], cpast[:, chn, :, :], [cpast.k()], [hs_.k('c')])

                def t_ident(jj, smp_part):
                    (j, chn, w0, w1, w2, cb, hfb, hs_, cyj) = prm[jj]
                    if not smp_part:
                        P.act(cyj[:, 0:NP_], hfb[:, 0:NP_], AF.Identity, [hfb.k(), sp.k()], [cyj.k('p')], bias=cb, scale=w0)
                    elif has_s:
                        oS = cyj[:, NP_:NP_ + 128].rearrange("p (s t) -> p s t", t=8)
                        P.act(oS, hs_[:, :, 0:8], AF.Identity, [hs_.k(), sp.k()], [cyj.k('s')], bias=cb, scale=w0)

                def t_stt():
                    for (j, chn, w0, w1, w2, cb, hfb, hs_, cyj) in prm:
                        oP = cyj[:, 0:NP_]
                        P.stt(oP, hfb[:, 1:NP_ + 1], w1, oP, ALU.mult, ALU.add, [hfb.k(), sp.k(), cyj.k('p')], [cyj.k('p')])
                        P.stt(oP, hfb[:, 2:NP_ + 2], w2, oP, ALU.mult, ALU.add, [hfb.k(), sp.k(), cyj.k('p')], [cyj.k('p')])
                        if hi == 0:
                            P.cp(hcar[:, chn, :], hfb[:, NP_:NP_ + 2], [hfb.k()], [hcar.k(chn)])
                        else:
                            P.cp(csp[:, chn, :], hfb[:, NP_:NP_ + 2], [hfb.k()], [csp.k(chn)])
                    if has_s:
                        for (j, chn, w0, w1, w2, cb, hfb, hs_, cyj) in prm:
                            oS = cyj[:, NP_:NP_ + 128].rearrange("p (s t) -> p s t", t=8)
                            P.stt(oS, hs_[:, :, 1:9], w1, oS, ALU.mult, ALU.add, [hs_.k(), sp.k(), cyj.k('s')], [cyj.k('s')])
                            P.stt(oS, hs_[:, :, 2:10], w2, oS, ALU.mult, ALU.add, [hs_.k(), sp.k(), cyj.k('s')], [cyj.k('s')])
                            P.cp(csout[:, chn, :, :], hs_[:, :, 8:10], [hs_.k()], [csout.k(chn)])

                def t_gelu():
                    P.act(cys[1][:, 0:NN], cys[1][:, 0:NN], AF.Gelu_apprx_tanh, [cys[1].k()], [cys[1].k()])

                def t_mult():
                    P.tt(gbuf[:, pi, 0:NN], cys[1][:, 0:NN], cys[0][:, 0:NN], ALU.mult, [cys[1].k(), cys[0].k()], [gbuf.k(pi)])
                return [t_carry, lambda: t_ident(0, False), lambda: t_ident(1, False), lambda: t_ident(0, True),
                        lambda: t_ident(1, True), t_stt, t_gelu, t_mult]

            for pc in mm_evac_pieces(0):
                pc()
            prev_tp = None
            for pi in range(NPAIR):
                tp = tail_pieces(pi)
                mp = mm_evac_pieces(pi + 1) if pi + 1 < NPAIR else []
                order = [tp[0]]
                ai = 0
                for t_ in tp[1:5]:
                    if ai < len(mp):
                        order.append(mp[ai])
                        ai += 1
                    order.append(t_)
                if prev_tp is not None:
                    order.append(prev_tp[6])
                while ai < len(mp):
                    order.append(mp[ai])
                    ai += 1
                order.append(tp[5])
                if prev_tp is not None:
                    order.append(prev_tp[7])
                for f_ in order:
                    f_()
                prev_tp = tp
                if pi == 1 and fin_hold[0] is not None:
                    fin_hold[0]()
                    fin_hold[0] = None
            prev_tp[6]()
            prev_tp[7]()
            goffs = {ti: (q * 512 if ti < 4 else NP_) for q, ti in enumerate(tl)}

            def mm_dn(ti, dc, pb, goffs=goffs):
                nonlocal di
                c0, n = TILES[ti]
                goff = goffs[ti]
                wd = wdn[di % 3]
                P.dma('pool', wd[:], w_dn.ap[l, :, :, dc * 128:(dc + 1) * 128], [w_dn.k()], [wd.k()], chan=f'wdn{di % 3}')
                di += 1
                for k in range(NPAIR):
                    P.mm(pb[:, 0:n], wd[:, k, :], gbuf[:, k, goff:goff + n], k == 0, k == NPAIR - 1, [wd.k(), gbuf.k(k)], [pb.k()])
            fin_ = resid_ln_seq(l, tl, mm_dn, mid, dst2, SP_LN2G, SP_LN2B, ln_m, 'p4', ('act' if l < L - 1 else 'none'))
            if hi == 0:
                fin_hold[0] = fin_
            else:
                fin_()
        P.dma('sp', o_conv_p.ap[l], csp[:], [csp.k()], [o_conv_p.k(l)], chan='csp')
        P.dma('sp', o_conv_s.ap[l], csout[:], [csout.k()], [o_conv_s.k(l)], chan='csout')
        P.release(base_mark)

    P.op('sp', lambda e: e.nop(), [], [o.k() for o in outs])


_CACHE = {}


def _get_program():
    if 'nc' not in _CACHE:
        _CACHE['nc'], _CACHE['stats'] = build_program()
    return _CACHE['nc']


def _kc(w):
    K, N = w.shape
    return np.ascontiguousarray(w.reshape(K // 128, 128, N).transpose(1, 0, 2))


def _chunkvec(v):
    return np.ascontiguousarray(v.reshape(-1, 128).T)


def _shared_inputs(inp):
    f = np.float32
    sh = {}
    sh['w_in'] = np.stack([_kc(inp['w_in'][l]) for l in range(L)])
    sh['w_br'] = np.stack([_kc(np.concatenate([inp['w_br_ssm'][l], inp['w_br_gmlp'][l], inp['w_br_pool'][l]], 0))
                           for l in range(L)])
    sh['w_o'] = np.stack([_kc(inp['w_o'][l]) for l in range(L)])
    sh['w_glu'] = np.stack([_kc(inp['ssm_w_glu'][l]) for l in range(L)])
    sh['w_up'] = np.stack([_kc(inp['ffn_w_up'][l]) for l in range(L)])
    sh['w_dn'] = np.stack([_kc(inp['ffn_w_down'][l]) for l in range(L)])
    sh['w_pool'] = np.ascontiguousarray(np.transpose(inp['pool_w'], (0, 2, 1, 3)))
    bpad = np.zeros((L, 128, 2, 16, 128), f)
    cpad = np.zeros((L, 128, 2, 16, 128), f)
    bpad2 = np.zeros((L, 128, 2, 16, 128), f)
    for ri, (bn, cn) in enumerate((('ssm_b_re', 'ssm_c_re'), ('ssm_b_im', 'ssm_c_im'))):
        B = inp[bn]
        C = inp[cn]
        for g in range(32):
            gp, g2 = g // 2, g % 2
            r0 = 32 * (gp % 4) + 16 * g2
            bpad[:, r0:r0 + 16, ri, gp, 64 * g2:64 * g2 + 64] = np.transpose(B[:, g], (0, 2, 1))
            cpad[:, 64 * g2:64 * g2 + 64, ri, gp, r0:r0 + 16] = np.transpose(C[:, g], (0, 2, 1))
            bpad2[:, 64 * g2:64 * g2 + 64, ri, gp, r0:r0 + 16] = B[:, g]
    sh['bpad'] = bpad
    sh['cpad'] = cpad
    sh['bpad2'] = bpad2
    wts = np.zeros((L, 128, 2, 4, 128), f)
    wts[:, :, 0] = np.transpose(inp['gmlp_w_s'], (0, 3, 1, 2))
    for s in range(16):
        wts[:, 8 * s:8 * s + 8, 1, :, 8 * s:8 * s + 8] = np.transpose(inp['gmlp_w_s'][:, :, :8, :8], (0, 3, 1, 2))
    sh['wts'] = wts
    sp = np.zeros((L, 128, NSP), f)
    bc = np.zeros((L, 128, NBC), f)
    for l in range(L):
        sp[l, :, SP_D:SP_D + 4] = _chunkvec(inp['ssm_d'][l].reshape(-1))
        sp[l, :, SP_BGLU:SP_BGLU + 4] = _chunkvec(inp['ssm_b_glu'][l])
        sp[l, :, SP_PSC:SP_PSC + 4] = _chunkvec(inp['pool_scale'][l])
        sp[l, :, SP_BG:SP_BG + 24] = _chunkvec(inp['b_gate'][l].reshape(-1))
        sp[l, :, SP_LN1G:SP_LN1G + 8] = _chunkvec(inp['ln1_g'][l])
        sp[l, :, SP_LN1B:SP_LN1B + 8] = _chunkvec(inp['ln1_b'][l])
        sp[l, :, SP_LN2G:SP_LN2G + 8] = _chunkvec(inp['ln2_g'][l])
        sp[l, :, SP_LN2B:SP_LN2B + 8] = _chunkvec(inp['ln2_b'][l])
        cw = inp['ffn_conv_w'][l]
        sp[l, :, SP_CW:SP_CW + 132] = np.transpose(cw.reshape(3, 44, 128), (2, 1, 0)).reshape(128, 132)
        sp[l, :, SP_CB:SP_CB + 44] = _chunkvec(inp['ffn_conv_b'][l])
        sp[l, :, SP_ARE:SP_ARE + 16] = inp['ssm_a_re'][l].reshape(16, 128).T
        sp[l, :, SP_AIM:SP_AIM + 16] = inp['ssm_a_im'][l].reshape(16, 128).T
        sp[l, :, SP_LS:SP_LS + 16] = np.repeat(inp['ssm_log_step'][l].reshape(16, 2), 64, axis=1).T
        bc[l, :, BC_LNG:BC_LNG + 256] = inp['gmlp_ln_g'][l][None, :]
        bc[l, :, BC_LNB:BC_LNB + 256] = inp['gmlp_ln_b'][l][None, :]
        bs = inp['gmlp_b_s'][l]
        for pr in range(2):
            for hh in range(2):
                bc[l, 64 * hh:64 * hh + 64, BC_BSP + pr * 128:BC_BSP + (pr + 1) * 128] = bs[2 * pr + hh][None, :]
                bc[l, 64 * hh:64 * hh + 64, BC_BSS + pr * 128:BC_BSS + (pr + 1) * 128] = np.tile(bs[2 * pr + hh, :8], 16)[None, :]
    sh['smallp'] = sp
    sh['bcp'] = bc
    rc = np.zeros((128, 4, 16), f)
    for g in range(4):
        rc[:, g, :] = (1.0 / np.minimum(np.arange(16) + 1, 2 ** (g + 1))).astype(f)[None, :]
    sh['rcc'] = rc
    return sh


def _prep(inp):
    sh = _shared_inputs(inp)
    in_maps = []
    for c in range(8):
        m = dict(sh)
        xp = inp['x_prompt'][c]
        xs = inp['x_sample'][16 * c:16 * c + 16].reshape(128, D)
        xa = np.concatenate([xp, xs], 0)
        m['xT0'] = np.ascontiguousarray(xa.reshape(NT, KC, 128).transpose(2, 1, 0))
        sl = slice(16 * c, 16 * c + 16)
        hr = inp['state_ssm_re'][:, sl].reshape(L, 16, 16, 128)
        hi = inp['state_ssm_im'][:, sl].reshape(L, 16, 16, 128)
        m['h0'] = np.ascontiguousarray(np.stack([hr, hi], 1).transpose(0, 4, 1, 3, 2))
        pp = inp['state_pool'][:, sl].reshape(L, 16, 15, 4, 128)
        m['poolpast'] = np.ascontiguousarray(pp.transpose(0, 4, 3, 1, 2))
        cp = inp['state_ffn_conv'][:, sl].reshape(L, 16, 2, 44, 128)
        m['convpast'] = np.ascontiguousarray(cp.transpose(0, 4, 3, 1, 2))
        in_maps.append(m)
    return in_maps


def kernel(**inp):
    inp = {k: np.asarray(v) for k, v in inp.items()}
    nc = _get_program()
    in_maps = _prep(inp)
    resu = run_bass_kernel_spmd(nc, in_maps, core_ids=list(range(8)))
    return _post(resu.results)


def _post(R):
    f = np.float32
    y_p = np.zeros((8, TP, D), f)
    y_s = np.zeros((128, 8, D), f)
    ssm_re_p = np.zeros((L, 8, 32, 64), f)
    ssm_im_p = np.zeros((L, 8, 32, 64), f)
    pool_p = np.zeros((L, 8, 15, 512), f)
    conv_p = np.zeros((L, 8, 2, F2), f)
    ssm_re_s = np.zeros((L, 128, 32, 64), f)
    ssm_im_s = np.zeros((L, 128, 32, 64), f)
    v_s = np.zeros((L, 128, 8, 256), f)
    pool_s = np.zeros((L, 128, 15, 512), f)
    conv_s = np.zeros((L, 128, 2, F2), f)
    for c in range(8):
        r = R[c]
        ya = r['yT'].transpose(2, 1, 0).reshape(NT, D)
        y_p[c] = ya[:TP]
        y_s[16 * c:16 * c + 16] = ya[TP:].reshape(16, 8, D)
        sl = slice(16 * c, 16 * c + 16)
        sp_ = r['o_ssm_p']
        ssm_re_p[:, c] = sp_[:, :, 0].transpose(0, 2, 1).reshape(L, 32, 64)
        ssm_im_p[:, c] = sp_[:, :, 1].transpose(0, 2, 1).reshape(L, 32, 64)
        ss = r['o_ssm_s']
        ssm_re_s[:, sl] = ss[:, :, 0].transpose(0, 3, 2, 1).reshape(L, 16, 32, 64)
        ssm_im_s[:, sl] = ss[:, :, 1].transpose(0, 3, 2, 1).reshape(L, 16, 32, 64)
        pool_p[:, c] = r['o_pool_p'].transpose(0, 3, 2, 1).reshape(L, 15, 512)
        pool_s[:, sl] = r['o_pool_s'].transpose(0, 3, 4, 2, 1).reshape(L, 16, 15, 512)
        conv_p[:, c] = r['o_conv_p'].transpose(0, 3, 2, 1).reshape(L, 2, F2)
        conv_s[:, sl] = r['o_conv_s'].transpose(0, 3, 4, 2, 1).reshape(L, 16, 2, F2)
        v_s[:, sl] = r['o_v_s'].reshape(L, 16, 8, 256)
    return (y_p, y_s, ssm_re_p, ssm_im_p, pool_p, conv_p, ssm_re_s, ssm_im_s, v_s, pool_s, conv_s)
```
